# Optimizing a Trainium2 kernel written in Bass

```python
import math
import jax
import jax.numpy as jnp
from jax import lax
import numpy as np

D_MODEL = 4096
BATCH = 1
SEQ = 8192
DEPTH = 1

MIX_W = D_MODEL
HEAD_DIM = 128
ATTN_W = MIX_W // 2
CONV_W = MIX_W - ATTN_W
N_ATTN_HEADS = ATTN_W // HEAD_DIM
N_CONV_GROUPS = CONV_W // HEAD_DIM
BLOCK = 256
TOPK = 3
QCHUNK = 32
CONV_WIDTH = 3
NUM_BUCKETS = 32
MAX_DISTANCE = 128
EPS = 1e-6
PROJ_W = 4 * ATTN_W + 4 * CONV_W

kernel_name = "hymba_moba_shortconv_t5bias"


def rmsnorm(x, g):
    xf = x.astype(jnp.float32)
    y = xf * lax.rsqrt(jnp.mean(xf * xf, axis=-1, keepdims=True) + EPS)
    return (y * g.astype(jnp.float32)).astype(x.dtype)


def rel_bucket(dist):
    n = jnp.maximum(dist, 0)
    max_exact = NUM_BUCKETS // 2
    nf = jnp.maximum(n, 1).astype(jnp.float32)
    large = max_exact + (jnp.log(nf / max_exact) / math.log(MAX_DISTANCE / max_exact)
                         * (NUM_BUCKETS - max_exact)).astype(jnp.int32)
    large = jnp.minimum(large, NUM_BUCKETS - 1)
    return jnp.where(n < max_exact, n, large)


def moba_attention(q, k, v, rel_bias):
    B, S, H, Dh = q.shape
    nb = -(-S // BLOCK)
    s_pad = nb * BLOCK
    pad = ((0, 0), (0, s_pad - S), (0, 0), (0, 0))
    kp = jnp.pad(k, pad)
    vp = jnp.pad(v, pad)
    kb = kp.reshape(B, nb, BLOCK, H, Dh)
    vb = vp.reshape(B, nb, BLOCK, H, Dh)
    kmean = jnp.mean(kb.astype(jnp.float32), axis=2)

    pos = jnp.arange(S)
    qblk = pos // BLOCK
    gate = jnp.einsum('bshd,bnhd->bhsn', q.astype(jnp.float32), kmean)
    past = jnp.arange(nb)[None, :] < qblk[:, None]
    gate = jnp.where(past[None, None], gate, -jnp.inf)
    k_eff = min(TOPK, nb)
    _, sel = lax.top_k(gate, k_eff)
    valid = jnp.arange(k_eff)[None, :] < qblk[:, None]

    kbh = kb.transpose(0, 3, 1, 2, 4)
    vbh = vb.transpose(0, 3, 1, 2, 4)
    bias_tab = rel_bias.T.astype(jnp.float32)
    b_i = jnp.arange(B)[:, None, None, None]
    h_i = jnp.arange(H)[None, :, None, None]
    kin = jnp.arange(BLOCK)
    scale = Dh ** -0.5

    def chunk(c):
        start = c * QCHUNK
        qpos = start + jnp.arange(QCHUNK)
        q_c = lax.dynamic_slice_in_dim(q, start, QCHUNK, 1)
        sel_c = lax.dynamic_slice_in_dim(sel, start, QCHUNK, 2)
        val_c = lax.dynamic_slice_in_dim(valid, start, QCHUNK, 0)
        k_sel = kbh[b_i, h_i, sel_c]
        v_sel = vbh[b_i, h_i, sel_c]
        kpos = sel_c[..., None] * BLOCK + kin
        dist = qpos[None, None, :, None, None] - kpos
        bias_s = bias_tab[h_i[..., None], rel_bucket(dist)]
        l_sel = jnp.einsum('bqhd,bhqnkd->bhqnk', q_c, k_sel).astype(jnp.float32) * scale + bias_s
        l_sel = jnp.where(val_c[None, None, :, :, None], l_sel, -jnp.inf)
        own = start // BLOCK
        k_own = lax.dynamic_slice_in_dim(kp, own * BLOCK, BLOCK, 1)
        v_own = lax.dynamic_slice_in_dim(vp, own * BLOCK, BLOCK, 1)
        dist_o = qpos[:, None] - (own * BLOCK + kin)[None, :]
        bias_o = bias_tab[:, rel_bucket(dist_o)]
        l_own = jnp.einsum('bqhd,bkhd->bhqk', q_c, k_own).astype(jnp.float32) * scale + bias_o[None]
        l_own = jnp.where((dist_o >= 0)[None, None], l_own, -jnp.inf)
        logits = jnp.concatenate([l_sel.reshape(B, H, QCHUNK, k_eff * BLOCK), l_own], axis=-1)
        p = jax.nn.softmax(logits, axis=-1).astype(v.dtype)
        p_sel = p[..., :k_eff * BLOCK].reshape(B, H, QCHUNK, k_eff, BLOCK)
        p_own = p[..., k_eff * BLOCK:]
        return (jnp.einsum('bhqnk,bhqnkd->bqhd', p_sel, v_sel)
                + jnp.einsum('bhqk,bkhd->bqhd', p_own, v_own))

    out = lax.map(chunk, jnp.arange(S // QCHUNK))
    return jnp.moveaxis(out, 0, 1).reshape(B, S, H, Dh)


def short_conv(u, w):
    S = u.shape[1]
    up = jnp.pad(u, ((0, 0), (CONV_WIDTH - 1, 0), (0, 0)))
    y = up[:, 0:S] * w[0]
    for j in range(1, CONV_WIDTH):
        y = y + up[:, j:j + S] * w[j]
    return y


def setup_inputs(seed: int = 0) -> dict:
    key = jax.random.key(seed)
    ks = jax.random.split(key, 8)
    x = jax.random.normal(ks[0], (BATCH, SEQ, D_MODEL), jnp.float32)
    norm_gain = 1.0 + 0.02 * jax.random.normal(ks[1], (DEPTH, D_MODEL), jnp.float32)
    w_in = jax.random.normal(ks[2], (DEPTH, D_MODEL, PROJ_W), jnp.float32) * D_MODEL ** -0.5
    conv_w = jax.random.normal(ks[3], (DEPTH, CONV_WIDTH, CONV_W), jnp.float32) * CONV_WIDTH ** -0.5
    w_out = jax.random.normal(ks[4], (DEPTH, ATTN_W + CONV_W, D_MODEL), jnp.float32) * (ATTN_W + CONV_W) ** -0.5
    rel_bias = 0.5 * jax.random.normal(ks[5], (NUM_BUCKETS, N_ATTN_HEADS), jnp.float32)
    final_gain = 1.0 + 0.02 * jax.random.normal(ks[6], (D_MODEL,), jnp.float32)
    return {"x": x, "norm_gain": norm_gain, "w_in": w_in, "conv_w": conv_w,
            "w_out": w_out, "rel_bias": rel_bias, "final_gain": final_gain}


def reference(x, norm_gain, w_in, conv_w, w_out, rel_bias, final_gain):
    B, S, _ = x.shape
    split_at = [ATTN_W, 2 * ATTN_W, 3 * ATTN_W, 4 * ATTN_W,
                4 * ATTN_W + CONV_W, 4 * ATTN_W + 2 * CONV_W, 4 * ATTN_W + 3 * CONV_W]
    h = x
    for l in range(DEPTH):
        u = rmsnorm(h, norm_gain[l])
        proj = jnp.einsum('bsd,df->bsf', u, w_in[l])
        q, k, v, z_a, hc, b_gate, c_gate, z_c = jnp.split(proj, split_at, axis=-1)
        q = q.reshape(B, S, N_ATTN_HEADS, HEAD_DIM)
        k = k.reshape(B, S, N_ATTN_HEADS, HEAD_DIM)
        v = v.reshape(B, S, N_ATTN_HEADS, HEAD_DIM)
        attn = moba_attention(q, k, v, rel_bias).reshape(B, S, ATTN_W) * jax.nn.silu(z_a)
        conv = b_gate * short_conv(c_gate * hc, conv_w[l]) * jax.nn.silu(z_c)
        mixed = jnp.concatenate([attn, conv], axis=-1)
        h = h + jnp.einsum('bsf,fd->bsd', mixed, w_out[l])
    return rmsnorm(h, final_gain)
```

```python
import math
from contextlib import ExitStack

import numpy as np
import ml_dtypes

import concourse.bass as bass
import concourse.mybir as mybir
from concourse.bass_utils import run_bass_kernel_spmd

F32 = mybir.dt.float32
BF16 = mybir.dt.bfloat16
ALU = mybir.AluOpType
AF = mybir.ActivationFunctionType

NCORES = 8
S = 8192
DM = 4096
KC = DM // 128
CH = 256
NCHUNK = S // CH
NBLK = 32
TOK_D = S // NCORES
EPS = 1e-6
SCALE = 128.0 ** -0.5
NEG = -30000.0


class Res:
    __slots__ = ("name", "w", "r", "dsem", "dkey", "dcnt", "excl")

    def __init__(self, name):
        self.name = name
        self.excl = False
        self.w = None
        self.r = {}
        self.dsem = None
        self.dkey = None
        self.dcnt = 0


class Eng:
    def __init__(self, e, sem, key, sync_self):
        self.e = e
        self.sem = sem
        self.key = key
        self.n = 0
        self.waited = {}
        self.sync_self = sync_self


class Trk:
    def __init__(self, nc, es, nsem=90):
        self.nc = nc
        self.sems = [es.enter_context(nc.semaphore(f"sm{i}")) for i in range(nsem)]
        self.next = 0
        self.engs = {}
        for name, e, ss in (("pe", nc.tensor, False), ("act", nc.scalar, True),
                            ("dve", nc.vector, True), ("pool", nc.gpsimd, True),
                            ("sp", nc.sync, True)):
            s, k = self.newsem()
            self.engs[name] = Eng(e, s, k, ss)
        self.dres = []
        self.limit = None
        self.nops = 0
        self.log = []

    def newsem(self):
        s = self.sems[self.next]
        self.next += 1
        return s, self.next

    def res(self, name, dma=False, excl=False):
        r = Res(name)
        r.excl = excl
        if dma:
            r.dsem, r.dkey = self.newsem()
            self.dres.append(r)
        return r

    def op(self, eng, fn, reads=(), writes=(), inc=True, dma=None):
        self.nops += 1
        if self.limit is not None and self.nops > self.limit:
            return None
        E = self.engs[eng]
        ex = [r for r in reads if r.excl]
        if ex:
            reads = [r for r in reads if not r.excl]
            writes = list(writes) + ex
        deps = {}

        def add(k, s, v):
            if k not in deps or deps[k][1] < v:
                deps[k] = (s, v)

        for r in reads:
            if r.w is not None:
                add(*r.w)
        for w in writes:
            if w.w is not None:
                add(*w.w)
            for k, (s, v) in w.r.items():
                add(k, s, v)
        for k, (s, v) in deps.items():
            if k == E.key and not E.sync_self:
                continue
            if E.waited.get(k, 0) >= v:
                continue
            E.e.wait_ge(s, v)
            E.waited[k] = v
        ins = fn()
        if dma is not None:
            dma.dcnt += 16
            ins.then_inc(dma.dsem, 16)
            ev = (dma.dkey, dma.dsem, dma.dcnt)
        elif inc:
            E.n += 1
            ins.then_inc(E.sem, 1)
            ev = (E.key, E.sem, E.n)
        else:
            ev = (E.key, E.sem, E.n + 1)
        for r in reads:
            k, s, v = ev
            if k not in r.r or r.r[k][1] < v:
                r.r[k] = (s, v)
        for w in writes:
            w.w = ev
            w.r = {}
        return ins

    def barrier(self, force=False):
        if self.limit is not None and self.nops > self.limit and not force:
            return
        for E in self.engs.values():
            for F in self.engs.values():
                if F is E or F.n == 0:
                    continue
                if E.waited.get(F.key, 0) < F.n:
                    E.e.wait_ge(F.sem, F.n)
                    E.waited[F.key] = F.n
            for r in self.dres:
                if r.dcnt and E.waited.get(r.dkey, 0) < r.dcnt:
                    E.e.wait_ge(r.dsem, r.dcnt)
                    E.waited[r.dkey] = r.dcnt


def build_nc(debug=False, phases="ABCD", ntok=S, limit=None):
    nc = bass.Bass("TRN2", target_bir_lowering=False)
    S = ntok
    NCHUNK = S // CH
    NT = S // 128
    NG = S // 512

    def dram_in(name, shape, dt):
        return nc.dram_tensor(name, shape, dt, kind="ExternalInput").ap()

    def dram_scr(name, shape, dt, dbg=True):
        if debug and dbg:
            return nc.dram_tensor(name, shape, dt, kind="ExternalOutput").ap()
        return nc.dram_tensor(name, shape, dt).ap()

    x = dram_in("x", [S, DM], F32)
    w_in = dram_in("w_in", [DM, 2048], F32)
    w_out = dram_in("w_out", [DM, 512], F32)
    x_cs = dram_in("x_cs", [S, 512], F32)
    gcol_d = dram_in("gcol", [128, KC], F32)
    convw_d = dram_in("convw", [128, 6], F32)
    relb_d = dram_in("relb", [128, 64], F32)
    fgs_d = dram_in("fgs", [128, 512], F32)
    bkp_d = dram_in("bk_prev", [128, 128], F32)
    bkd_d = dram_in("bk_diag", [128, 128], F32)
    cmd_d = dram_in("cm_diag", [128, 128], F32)
    ident_d = dram_in("ident", [128, 128], BF16)
    out = nc.dram_tensor("out", [S, 512], F32, kind="ExternalOutput").ap()
    ari_t = nc.dram_tensor("ar_in", [128, 64], F32)
    aro_t = nc.dram_tensor("ar_out", [128, 64], F32)

    qT_d = dram_scr("qT_s", [2, 128, S], BF16)
    kT_d = dram_scr("kT_s", [2, 128, S], BF16)
    v_d = dram_scr("v_s", [S, 256], BF16)
    sza_d = dram_scr("sza_s", [S, 256], F32)
    mask_d = dram_scr("mask_s", [S, 64], F32)
    gci_t = nc.dram_tensor("gc_in", [256, S], BF16)
    gh_t = [nc.dram_tensor(f"gh{h}_in", [128, S], BF16) for h in range(2)]
    gco_t = nc.dram_tensor("gc_out", [NCORES * 256, S], BF16)
    gho_t = [nc.dram_tensor(f"gh{h}_out", [NCORES * 128, S], BF16) for h in range(2)]
    if debug:
        dbg_gci = nc.dram_tensor("dbg_gci", [256, S], BF16, kind="ExternalOutput").ap()
        dbg_gh = [nc.dram_tensor(f"dbg_gh{h}", [128, S], BF16, kind="ExternalOutput").ap()
                  for h in range(2)]

    with ExitStack() as es:
        T = Trk(nc, es)
        T.limit = limit
        nc._trk = T
        op = T.op
        pe, act, dve, pool, sp = nc.tensor, nc.scalar, nc.vector, nc.gpsimd, nc.sync

        def sb(st, name, shape, dt):
            return st.enter_context(nc.sbuf_tensor("s_" + name, shape, dt))

        def ps(st, name, shape, dt):
            return st.enter_context(nc.psum_tensor("p_" + name, shape, dt))

        R_qT = [T.res(f"qT{h}") for h in range(2)]
        R_kT = [T.res(f"kT{h}") for h in range(2)]
        R_v = T.res("v_d")
        R_sza = T.res("sza_d")
        R_mask = T.res("mask_d")
        R_gci = T.res("gci")
        R_gh = [T.res(f"gh{h}") for h in range(2)]
        R_gco = T.res("gco")
        R_gho = [T.res(f"gho{h}") for h in range(2)]
        R_out = T.res("out")

        ident = sb(es, "ident", [128, 128], BF16)
        gcol = sb(es, "gcol", [128, KC], F32)
        convw = sb(es, "convw", [128, 6], F32)
        relb = sb(es, "relb", [128, 64], F32)
        neghalf = sb(es, "neghalf", [128, 1], F32)
        R_const = T.res("const", dma=True)
        R_nh = T.res("neghalf")
        for dst, src in ((ident, ident_d), (gcol, gcol_d), (convw, convw_d), (relb, relb_d)):
            op("sp", lambda dst=dst, src=src: sp.dma_start(out=dst[:], in_=src), writes=[R_const],
               dma=R_const)

        if "A" in phases:
          with ExitStack() as st:
            Wb = sb(st, "Wb", [128, KC, 2048], BF16)
            xh = [sb(st, f"xh{i}", [128, 2048], F32) for i in range(3)]
            xs = sb(st, "xs", [128, DM], BF16)
            uT = [sb(st, f"uT{i}", [128, KC, CH], BF16) for i in range(2)]
            ss = [sb(st, f"ss{i}", [128, 2], F32) for i in range(2)]
            ms = [sb(st, f"ms{i}", [128, 1], F32) for i in range(2)]
            rstd = [sb(st, f"rstd{i}", [128, 1], F32) for i in range(2)]
            qs = [sb(st, f"qs{i}", [128, CH], BF16) for i in range(2)]
            ks = [sb(st, f"ks{i}", [128, CH], BF16) for i in range(2)]
            ksum = [sb(st, f"ksum{i}", [128, 1], F32) for i in range(2)]
            kmT = [sb(st, f"kmT{i}", [128, NBLK], BF16) for i in range(2)]
            vs = [sb(st, f"vs{i}", [128, 256], BF16) for i in range(2)]
            sz1 = sb(st, "sz", [128, 256], F32)
            sz = [sz1, sz1]
            gsb = [sb(st, f"gsb{i}", [128, NBLK], F32) for i in range(4)]
            top8 = sb(st, "top8", [128, 8], F32)
            thr = sb(st, "thr", [128, 1], F32)
            msk = [sb(st, f"msk{i}", [128, 64], F32) for i in range(2)]
            Csb = sb(st, "Csb", [128, CH], F32)
            t1 = [sb(st, f"t1_{i}", [128, CH + 2], F32) for i in range(2)]
            yy = sb(st, "yy", [128, CH], F32)
            szc = sb(st, "szc", [128, CH], F32)
            cv = [sb(st, f"cv{i}", [128, CH], BF16) for i in range(2)]
            tp = [ps(st, f"tp{i}", [128, 8, 128], BF16) for i in range(2)]
            fm = [ps(st, f"fm{i}", [128, 512], F32) for i in range(4)]
            tm = [ps(st, f"tm{i}", [128, 512], F32) for i in range(2)]

            R_W = [T.res(f"W{k}") for k in range(KC)]
            R_xh = [T.res(f"xh{i}", dma=True) for i in range(3)]
            R_xs = [T.res("xs0"), T.res("xs1")]
            R_uT = [[T.res(f"uT{i}_{g}") for g in range(4)] for i in range(2)]
            R_ss = [T.res(f"ss{i}") for i in range(2)]
            R_ms = [T.res(f"ms{i}") for i in range(2)]
            R_rstd = [T.res(f"rstd{i}") for i in range(2)]
            R_qs = [T.res(f"qs{i}", dma=True) for i in range(2)]
            R_ks = [T.res(f"ks{i}", dma=True) for i in range(2)]
            R_ksum = [T.res(f"ksum{i}") for i in range(2)]
            R_kmT = [T.res(f"kmT{i}") for i in range(2)]
            R_vs = [T.res(f"vs{i}", dma=True) for i in range(2)]
            R_sz1 = T.res("sz", dma=True)
            R_sz = [R_sz1, R_sz1]
            R_gsb = [T.res(f"gsb{i}") for i in range(4)]
            R_top8 = T.res("top8")
            R_thr = T.res("thr")
            R_msk = [T.res(f"msk{i}", dma=True) for i in range(2)]
            R_Csb = T.res("Csb")
            R_t1 = [T.res(f"t1_{i}") for i in range(2)]
            R_yy = T.res("yy")
            R_szc = T.res("szc")
            R_cv = [T.res(f"cv{i}", dma=True) for i in range(2)]
            R_tp = [T.res(f"tp{i}", excl=True) for i in range(2)]
            R_fm = [T.res(f"fm{i}", excl=True) for i in range(4)]
            R_tm = [T.res(f"tm{i}", excl=True) for i in range(2)]

            for i in range(2):
                op("dve", lambda i=i: dve.memset(kmT[i][:], 0.0), writes=[R_kmT[i]])
                op("dve", lambda i=i: dve.memset(t1[i][:], 0.0), writes=[R_t1[i]])
            for i in range(4):
                op("dve", lambda i=i: dve.memset(gsb[i][:], -1e30), writes=[R_gsb[i]])

            for k in range(KC):
                s_ = k % 3
                op("sp", lambda k=k, s_=s_: sp.dma_start(out=xh[s_][:], in_=w_in[k * 128:(k + 1) * 128, :]),
                   writes=[R_xh[s_]], dma=R_xh[s_])
                if k % 2 == 0:
                    op("dve", lambda k=k, s_=s_: dve.tensor_scalar(
                        out=Wb[:, k, :], in0=xh[s_][:], scalar1=gcol[:, k:k + 1], scalar2=None,
                        op0=ALU.mult), reads=[R_xh[s_], R_const], writes=[R_W[k]])
                else:
                    op("act", lambda k=k, s_=s_: act.activation(
                        out=Wb[:, k, :], in_=xh[s_][:], func=AF.Copy, scale=gcol[:, k:k + 1]),
                       reads=[R_xh[s_], R_const], writes=[R_W[k]])

            slot_ctr = [0]
            fm_ctr = [0]

            def prep(c):
                u = c % 2
                for tt in range(2):
                    i = 2 * c + tt
                    b = i % 2
                    slots = []
                    for h in range(2):
                        s_ = slot_ctr[0] % 3
                        slot_ctr[0] += 1
                        slots.append(s_)
                        op("sp", lambda i=i, h=h, s_=s_: sp.dma_start(
                            out=xh[s_][:], in_=x[i * 128:(i + 1) * 128, h * 2048:(h + 1) * 2048]),
                           writes=[R_xh[s_]], dma=R_xh[s_])
                        op("act", lambda h=h, s_=s_, b=b: act.activation(
                            out=xs[:, h * 2048:(h + 1) * 2048], in_=xh[s_][:], func=AF.Square,
                            accum_out=ss[b][:, h:h + 1]),
                           reads=[R_xh[s_]], writes=[R_xs[h], R_ss[b]])
                    op("dve", lambda b=b: dve.tensor_tensor(
                        out=ms[b][:], in0=ss[b][:, 0:1], in1=ss[b][:, 1:2], op=ALU.add),
                       reads=[R_ss[b]], writes=[R_ms[b]])
                    op("act", lambda b=b: act.activation(
                        out=ms[b][:], in_=ms[b][:], func=AF.Sqrt, bias=EPS, scale=1.0 / DM),
                       reads=[R_ms[b]], writes=[R_ms[b]])
                    op("dve", lambda b=b: dve.reciprocal(out=rstd[b][:], in_=ms[b][:]),
                       reads=[R_ms[b]], writes=[R_rstd[b]])
                    for h in range(2):
                        s_ = slots[h]
                        op("act", lambda h=h, s_=s_, b=b: act.activation(
                            out=xs[:, h * 2048:(h + 1) * 2048], in_=xh[s_][:], func=AF.Copy,
                            scale=rstd[b][:, 0:1]),
                           reads=[R_xh[s_], R_rstd[b]], writes=[R_xs[h]])
                    for kg in range(4):
                        pb = kg % 2
                        for kk in range(8):
                            k = kg * 8 + kk
                            op("pe", lambda k=k, kk=kk, pb=pb: pe.transpose(
                                out=tp[pb][:, kk, :], in_=xs[:, k * 128:(k + 1) * 128],
                                identity=ident[:]),
                               reads=[R_xs[k // 16], R_const], writes=[R_tp[pb]], inc=(kk == 7))
                        op("dve", lambda kg=kg, pb=pb, u=u, tt=tt: dve.tensor_copy(
                            out=uT[u][:, kg * 8:(kg + 1) * 8, tt * 128:(tt + 1) * 128],
                            in_=tp[pb][:]),
                           reads=[R_tp[pb]], writes=[R_uT[u][kg]])

            def fm_group(c, col0, evac):
                u = c % 2
                bi = fm_ctr[0] % 4
                fm_ctr[0] += 1
                for k in range(KC):
                    op("pe", lambda k=k, bi=bi, u=u: pe.matmul(
                        out=fm[bi][:, 0:CH], lhsT=Wb[:, k, col0:col0 + 128], rhs=uT[u][:, k, :],
                        start=(k == 0), stop=(k == KC - 1)),
                       reads=[R_W[k], R_uT[u][k // 8]], writes=[R_fm[bi]], inc=(k == KC - 1))
                evac(bi)

            def inproj(c):
                u = c % 2
                for h in range(2):
                    def ev_q(bi, h=h):
                        op("dve", lambda: dve.tensor_copy(out=qs[h][:], in_=fm[bi][:, 0:CH]),
                           reads=[R_fm[bi]], writes=[R_qs[h]])
                        op("pool", lambda: pool.dma_start(
                            out=qT_d[h, :, c * CH:(c + 1) * CH], in_=qs[h][:]),
                           reads=[R_qs[h]], writes=[R_qT[h]], dma=R_qs[h])
                    fm_group(c, h * 128, ev_q)
                for h in range(2):
                    def ev_k(bi, h=h):
                        op("act", lambda: act.activation(
                            out=ks[h][:], in_=fm[bi][:, 0:CH], func=AF.Copy,
                            accum_out=ksum[h][:, 0:1]),
                           reads=[R_fm[bi]], writes=[R_ks[h], R_ksum[h]])
                        op("pool", lambda: pool.dma_start(
                            out=kT_d[h, :, c * CH:(c + 1) * CH], in_=ks[h][:]),
                           reads=[R_ks[h]], writes=[R_kT[h]], dma=R_ks[h])
                    fm_group(c, 256 + h * 128, ev_k)
                bi = fm_ctr[0] % 4
                fm_ctr[0] += 1
                for h in range(2):
                    for tt in range(2):
                        gi = h * 2 + tt
                        op("pe", lambda h=h, tt=tt, gi=gi, bi=bi: pe.matmul(
                            out=fm[bi][:, gi * 32:(gi + 1) * 32],
                            lhsT=qs[h][:, tt * 128:(tt + 1) * 128], rhs=kmT[h][:],
                            start=True, stop=True),
                           reads=[R_qs[h], R_kmT[h]], writes=[R_fm[bi]], inc=(gi == 3))
                for tt in range(2):
                    for h in range(2):
                        gi = h * 2 + tt
                        if c >= 1:
                            op("dve", lambda gi=gi, bi=bi: dve.tensor_copy(
                                out=gsb[gi][:, 0:c], in_=fm[bi][:, gi * 32:gi * 32 + c]),
                               reads=[R_fm[bi]], writes=[R_gsb[gi]])
                        op("dve", lambda gi=gi: dve.max(out=top8[:], in_=gsb[gi][:]),
                           reads=[R_gsb[gi]], writes=[R_top8])
                        op("dve", lambda: dve.tensor_scalar(
                            out=thr[:], in0=top8[:, 2:3], scalar1=-1e29, scalar2=None, op0=ALU.max),
                           reads=[R_top8], writes=[R_thr])
                        op("dve", lambda gi=gi, h=h, tt=tt: dve.tensor_scalar(
                            out=msk[tt][:, h * 32:(h + 1) * 32], in0=gsb[gi][:],
                            scalar1=thr[:, 0:1], scalar2=None, op0=ALU.is_ge),
                           reads=[R_gsb[gi], R_thr], writes=[R_msk[tt]])
                    op("pool", lambda tt=tt: pool.dma_start(
                        out=mask_d[(2 * c + tt) * 128:(2 * c + tt + 1) * 128, :], in_=msk[tt][:]),
                       reads=[R_msk[tt]], writes=[R_mask], dma=R_msk[tt])
                for h in range(2):
                    op("dve", lambda h=h: dve.tensor_scalar(
                        out=kmT[h][:, c:c + 1], in0=ksum[h][:, 0:1], scalar1=1.0 / CH,
                        scalar2=None, op0=ALU.mult),
                       reads=[R_ksum[h]], writes=[R_kmT[h]])
                for j in range(2):
                    def ev_C(bi):
                        op("act", lambda: act.activation(out=Csb[:], in_=fm[bi][:, 0:CH], func=AF.Copy),
                           reads=[R_fm[bi]], writes=[R_Csb])
                    fm_group(c, 768 + j * 128, ev_C)

                    def ev_hc(bi, j=j):
                        op("dve", lambda: dve.tensor_tensor(
                            out=t1[j][:, 2:CH + 2], in0=fm[bi][:, 0:CH], in1=Csb[:], op=ALU.mult),
                           reads=[R_fm[bi], R_Csb], writes=[R_t1[j]])
                        op("dve", lambda: dve.tensor_scalar(
                            out=yy[:], in0=t1[j][:, 0:CH], scalar1=convw[:, j * 3:j * 3 + 1],
                            scalar2=None, op0=ALU.mult),
                           reads=[R_t1[j], R_const], writes=[R_yy])
                        for tap in (1, 2):
                            op("dve", lambda tap=tap: dve.scalar_tensor_tensor(
                                out=yy[:], in0=t1[j][:, tap:tap + CH],
                                scalar=convw[:, j * 3 + tap:j * 3 + tap + 1], in1=yy[:],
                                op0=ALU.mult, op1=ALU.add),
                               reads=[R_t1[j], R_const, R_yy], writes=[R_yy])
                        op("dve", lambda: dve.tensor_copy(out=t1[j][:, 0:2], in_=t1[j][:, CH:CH + 2]),
                           reads=[R_t1[j]], writes=[R_t1[j]])
                    fm_group(c, 512 + j * 128, ev_hc)

                    def ev_B(bi):
                        op("dve", lambda: dve.tensor_tensor(
                            out=Csb[:], in0=fm[bi][:, 0:CH], in1=yy[:], op=ALU.mult),
                           reads=[R_fm[bi], R_yy], writes=[R_Csb])
                    fm_group(c, 1024 + j * 128, ev_B)

                    def ev_zc(bi, j=j):
                        op("act", lambda: act.activation(out=szc[:], in_=fm[bi][:, 0:CH], func=AF.Silu),
                           reads=[R_fm[bi]], writes=[R_szc])
                        op("dve", lambda: dve.tensor_tensor(
                            out=cv[j][:], in0=Csb[:], in1=szc[:], op=ALU.mult),
                           reads=[R_Csb, R_szc], writes=[R_cv[j]])
                        op("pool", lambda: pool.dma_start(
                            out=gci_t.ap()[j * 128:(j + 1) * 128, c * CH:(c + 1) * CH], in_=cv[j][:]),
                           reads=[R_cv[j]], writes=[R_gci], dma=R_cv[j])
                    fm_group(c, 1280 + j * 128, ev_zc)
                for tt in range(2):
                    for k in range(KC):
                        op("pe", lambda k=k, tt=tt: pe.matmul(
                            out=tm[tt][:, :], lhsT=uT[u][:, k, tt * 128:(tt + 1) * 128],
                            rhs=Wb[:, k, 1536:2048], start=(k == 0), stop=(k == KC - 1)),
                           reads=[R_W[k], R_uT[u][k // 8]], writes=[R_tm[tt]], inc=(k == KC - 1))
                    row0 = (2 * c + tt) * 128
                    op("dve", lambda tt=tt: dve.tensor_copy(out=vs[tt][:], in_=tm[tt][:, 0:256]),
                       reads=[R_tm[tt]], writes=[R_vs[tt]])
                    op("pool", lambda tt=tt, row0=row0: pool.dma_start(
                        out=v_d[row0:row0 + 128, :], in_=vs[tt][:]),
                       reads=[R_vs[tt]], writes=[R_v], dma=R_vs[tt])
                    op("act", lambda tt=tt: act.activation(out=sz[tt][:], in_=tm[tt][:, 256:512], func=AF.Silu),
                       reads=[R_tm[tt]], writes=[R_sz[tt]])
                    op("pool", lambda tt=tt, row0=row0: pool.dma_start(
                        out=sza_d[row0:row0 + 128, :], in_=sz[tt][:]),
                       reads=[R_sz[tt]], writes=[R_sza], dma=R_sz[tt])

            prep(0)
            for c in range(NCHUNK):
                if c + 1 < NCHUNK:
                    prep(c + 1)
                inproj(c)
            T.barrier()


        if "B" in phases:
          with ExitStack() as st:
            KT2 = [sb(st, f"KT{i}", [128, S], BF16) for i in range(2)]
            QT2 = [sb(st, f"QT{i}", [128, S], BF16) for i in range(2)]
            VA2 = [sb(st, f"VA{i}", [128, NT, 130], BF16) for i in range(2)]
            MK2 = [sb(st, f"MK{i}", [128, NT, 32], F32) for i in range(2)]
            SZ = [sb(st, f"SZ{i}", [128, 4, 128], F32) for i in range(2)]
            NPT = 6
            PT = [sb(st, f"PT{i}", [128, 512], BF16) for i in range(NPT)]
            NSS = 4
            Ssb = [sb(st, f"Ssb{i}", [128, 128], F32) for i in range(NSS)]
            ACC = [sb(st, f"ACC{i}", [128, 4, 130], F32) for i in range(2)]
            rec = sb(st, "rec", [128, 1], F32)
            og = [sb(st, f"og{i}", [128, 128], BF16) for i in range(2)]
            aT = [sb(st, f"aT{i}", [128, 512], BF16) for i in range(2)]
            NSB = 3
            SB = [ps(st, f"SB{i}", [128, 512], F32) for i in range(NSB)]
            NOB = 3
            OB = [ps(st, f"OB{i}", [128, 512], F32) for i in range(NOB)]
            TPB = ps(st, "TPB", [128, 1024], BF16)

            R_KT2 = [T.res(f"KT{i}", dma=True) for i in range(2)]
            R_QT2 = [T.res(f"QT{i}", dma=True) for i in range(2)]
            R_VA2 = [T.res(f"VA{i}", dma=True) for i in range(2)]
            R_MK2 = [T.res(f"MK{i}", dma=True) for i in range(2)]
            R_SZ = [T.res(f"SZ{i}", dma=True) for i in range(2)]
            R_PT = [T.res(f"PT{i}") for i in range(NPT)]
            R_Ssb = [T.res(f"Ssb{i}") for i in range(NSS)]
            R_ACC = [[T.res(f"ACC{i}_{q}") for q in range(4)] for i in range(2)]
            R_rec = T.res("rec")
            R_og = [T.res(f"og{i}") for i in range(2)]
            R_aT = [T.res(f"aT{i}", dma=True) for i in range(2)]
            R_SB = [T.res(f"SB{i}", excl=True) for i in range(NSB)]
            R_OB = [T.res(f"OB{i}", excl=True) for i in range(NOB)]
            R_TPB = T.res("TPB", excl=True)
            TT = sb(st, "TT", [128, 4, 128], F32)
            R_TT = T.res("TT")
            if True:
                bk = sb(st, "bk", [128, 3, 128], F32)
                tmp = sb(st, "tt_tmp", [128, 128], F32)
                R_bk = T.res("bk", dma=True)
                R_tmp = T.res("tt_tmp")
                for i, src in enumerate((bkp_d, bkd_d, cmd_d)):
                    op("sp", lambda i=i, src=src: sp.dma_start(out=bk[:, i, :], in_=src),
                       dma=R_bk)
                R_bk.w = (R_bk.dkey, R_bk.dsem, R_bk.dcnt)
                for h in range(2):
                    for kind in range(2):
                        dstt = TT[:, h * 2 + kind, :]
                        for b in range(32):
                            if b == 0:
                                op("dve", lambda dstt=dstt, kind=kind, h=h, b=b: dve.tensor_scalar(
                                    out=dstt, in0=bk[:, kind, :], scalar1=float(b),
                                    scalar2=relb[:, h * 32 + b:h * 32 + b + 1],
                                    op0=ALU.is_equal, op1=ALU.mult),
                                   reads=[R_bk, R_const], writes=[R_TT])
                            else:
                                op("dve", lambda kind=kind, h=h, b=b: dve.tensor_scalar(
                                    out=tmp[:], in0=bk[:, kind, :], scalar1=float(b),
                                    scalar2=relb[:, h * 32 + b:h * 32 + b + 1],
                                    op0=ALU.is_equal, op1=ALU.mult),
                                   reads=[R_bk, R_const], writes=[R_tmp])
                                op("dve", lambda dstt=dstt: dve.tensor_tensor(
                                    out=dstt, in0=dstt, in1=tmp[:], op=ALU.add),
                                   reads=[R_tmp, R_TT], writes=[R_TT])
                        if kind == 1:
                            op("dve", lambda dstt=dstt: dve.tensor_tensor(
                                out=dstt, in0=dstt, in1=bk[:, 2, :], op=ALU.add),
                               reads=[R_bk, R_TT], writes=[R_TT])


            for i in range(2):
                op("dve", lambda i=i: dve.memset(VA2[i][:, :, 128:130], 1.0), writes=[R_VA2[i]])
            v_r = v_d.rearrange("(t p) d -> p t d", p=128)
            m_r = mask_d.rearrange("(t p) m -> p t m", p=128)
            z_r = sza_d.rearrange("(t p) d -> p t d", p=128)

            ctr = {"pt": 0, "ss": 0, "sb": 0, "ob": 0, "og": 0, "sz": 0}

            def load_head(hh, extra=()):
                KT, QT, VA, MK = KT2[hh], QT2[hh], VA2[hh], MK2[hh]
                R_KT, R_QT, R_VA, R_MK = R_KT2[hh], R_QT2[hh], R_VA2[hh], R_MK2[hh]
                extra = list(extra)
                for q4 in range(4):
                    sl = slice(q4 * (S // 4), (q4 + 1) * (S // 4))
                    op("sp", lambda sl=sl: sp.dma_start(out=KT[:, sl], in_=kT_d[hh, :, sl]),
                       reads=[R_kT[hh]] + extra, writes=[R_KT], dma=R_KT)
                    op("sp", lambda sl=sl: sp.dma_start(out=QT[:, sl], in_=qT_d[hh, :, sl]),
                       reads=[R_qT[hh]], writes=[R_QT], dma=R_QT)
                for t8 in range(8):
                    op("sp", lambda t8=t8: sp.dma_start(
                        out=VA[:, t8 * (NT // 8):(t8 + 1) * (NT // 8), 0:128],
                        in_=v_r[:, t8 * (NT // 8):(t8 + 1) * (NT // 8), hh * 128:(hh + 1) * 128]),
                       reads=[R_v], writes=[R_VA], dma=R_VA)
                for t4 in range(4):
                    op("sp", lambda t4=t4: sp.dma_start(
                        out=MK[:, t4 * (NT // 4):(t4 + 1) * (NT // 4), :],
                        in_=m_r[:, t4 * (NT // 4):(t4 + 1) * (NT // 4), hh * 32:(hh + 1) * 32]),
                       reads=[R_mask], writes=[R_MK], dma=R_MK)
                for r_ in (R_KT, R_QT, R_VA, R_MK):
                    r_.w = (r_.dkey, r_.dsem, r_.dcnt)

            load_head(0)
            if "C" in phases:
                op("pool", lambda: pool.collective_compute(
                    "AllGather", ALU.bypass, replica_groups=[list(range(NCORES))],
                    ins=[gci_t.ap()], outs=[gco_t.ap()]),
                   reads=[R_gci, R_KT2[0], R_QT2[0], R_VA2[0], R_MK2[0]], writes=[R_gco])

            for hh in range(2):
                KT, QT, VA, MK = KT2[hh], QT2[hh], VA2[hh], MK2[hh]
                R_KT, R_QT, R_VA, R_MK = R_KT2[hh], R_QT2[hh], R_VA2[hh], R_MK2[hh]
                c31 = relb[:, hh * 32 + 31:hh * 32 + 32]
                TTp = TT[:, hh * 2 + 0, :]
                TTd = TT[:, hh * 2 + 1, :]

                items = []
                for g in range(NG):
                    for j in range(2 * g + 2):
                        kts = []
                        for half in range(2):
                            kt = 2 * j + half
                            segs = []
                            for qi in range(4):
                                qt = 4 * g + qi
                                if kt > qt:
                                    continue
                                if kt == qt:
                                    segs.append((qi, "diag"))
                                elif kt == qt - 1:
                                    segs.append((qi, "prev"))
                                else:
                                    segs.append((qi, "far"))
                            if segs:
                                kts.append((kt, segs))
                        items.append((g, j, kts))

                state = {}

                def qk_exp(item):
                    g, j, kts = item
                    outl = []
                    for kt, segs in kts:
                        q0 = segs[0][0]
                        ncol = (4 - q0) * 128
                        bi = ctr["sb"] % NSB
                        ctr["sb"] += 1
                        pi = ctr["pt"] % NPT
                        ctr["pt"] += 1
                        op("pe", lambda kt=kt, q0=q0, ncol=ncol, bi=bi, g=g: pe.matmul(
                            out=SB[bi][:, q0 * 128:512], lhsT=KT[:, kt * 128:(kt + 1) * 128],
                            rhs=QT[:, g * 512 + q0 * 128:(g + 1) * 512], start=True, stop=True),
                           reads=[R_KT, R_QT], writes=[R_SB[bi]])
                        far0 = None
                        for qi, kind in segs:
                            if kind == "far":
                                if far0 is None:
                                    far0 = qi
                                continue
                            si = ctr["ss"] % NSS
                            ctr["ss"] += 1
                            tt_ = TTd if kind == "diag" else TTp
                            cs = slice(qi * 128, (qi + 1) * 128)
                            op("dve", lambda bi=bi, si=si, tt_=tt_, cs=cs: dve.scalar_tensor_tensor(
                                out=Ssb[si][:], in0=SB[bi][:, cs], scalar=SCALE, in1=tt_,
                                op0=ALU.mult, op1=ALU.add),
                               reads=[R_SB[bi], R_TT], writes=[R_Ssb[si]])
                            op("act", lambda si=si, pi=pi, cs=cs: act.activation(
                                out=PT[pi][:, cs], in_=Ssb[si][:], func=AF.Exp),
                               reads=[R_Ssb[si]], writes=[R_PT[pi]])
                        if far0 is not None:
                            cs = slice(far0 * 128, 512)
                            op("act", lambda bi=bi, pi=pi, cs=cs: act.activation(
                                out=PT[pi][:, cs], in_=SB[bi][:, cs], func=AF.Exp, bias=c31,
                                scale=SCALE),
                               reads=[R_SB[bi], R_const], writes=[R_PT[pi]])
                        outl.append((kt, pi, [s[0] for s in segs]))
                    return outl

                def pv(item, ptl):
                    g, j, kts = item
                    a = g % 2
                    for qi in range(4):
                        qt = 4 * g + qi
                        parts = [(kt, pi) for kt, pi, qis in ptl if qi in qis]
                        if not parts:
                            continue
                        oi = ctr["ob"] % NOB
                        ctr["ob"] += 1
                        for n_, (kt, pi) in enumerate(parts):
                            op("pe", lambda kt=kt, pi=pi, oi=oi, qi=qi, n_=n_, L=len(parts): pe.matmul(
                                out=OB[oi][:, 0:129], lhsT=PT[pi][:, qi * 128:(qi + 1) * 128],
                                rhs=VA[:, kt, 0:129], start=(n_ == 0), stop=(n_ == L - 1)),
                               reads=[R_PT[pi], R_VA], writes=[R_OB[oi]], inc=(n_ == len(parts) - 1))
                        own = (j == qt // 2)
                        first = (g, qi) not in state
                        state[(g, qi)] = True
                        accv = ACC[a][:, qi, 0:129]
                        mcol = MK[:, qt, j:j + 1]
                        if first and own:
                            op("dve", lambda oi=oi, accv=accv: dve.tensor_copy(out=accv, in_=OB[oi][:, 0:129]),
                               reads=[R_OB[oi]], writes=[R_ACC[a][qi]])
                        elif first:
                            op("dve", lambda oi=oi, accv=accv, mcol=mcol: dve.tensor_scalar(
                                out=accv, in0=OB[oi][:, 0:129], scalar1=mcol, scalar2=None, op0=ALU.mult),
                               reads=[R_OB[oi], R_MK], writes=[R_ACC[a][qi]])
                        elif own:
                            op("dve", lambda oi=oi, accv=accv: dve.tensor_tensor(
                                out=accv, in0=OB[oi][:, 0:129], in1=accv, op=ALU.add),
                               reads=[R_OB[oi], R_ACC[a][qi]], writes=[R_ACC[a][qi]])
                        else:
                            op("dve", lambda oi=oi, accv=accv, mcol=mcol: dve.scalar_tensor_tensor(
                                out=accv, in0=OB[oi][:, 0:129], scalar=mcol, in1=accv,
                                op0=ALU.mult, op1=ALU.add),
                               reads=[R_OB[oi], R_MK, R_ACC[a][qi]], writes=[R_ACC[a][qi]])

                def finalize(g):
                    a = g % 2
                    zi = ctr["sz"] % 2
                    ctr["sz"] += 1
                    op("sp", lambda zi=zi, g=g: sp.dma_start(
                        out=SZ[zi][:], in_=z_r[:, 4 * g:4 * g + 4, hh * 128:(hh + 1) * 128]),
                       reads=[R_sza], writes=[R_SZ[zi]], dma=R_SZ[zi])
                    for qi in range(4):
                        o_ = ctr["og"] % 2
                        ctr["og"] += 1
                        op("dve", lambda a=a, qi=qi: dve.reciprocal(out=rec[:], in_=ACC[a][:, qi, 128:129]),
                           reads=[R_ACC[a][qi]], writes=[R_rec])
                        op("dve", lambda a=a, qi=qi, o_=o_, zi=zi: dve.scalar_tensor_tensor(
                            out=og[o_][:], in0=ACC[a][:, qi, 0:128], scalar=rec[:, 0:1],
                            in1=SZ[zi][:, qi, :], op0=ALU.mult, op1=ALU.mult),
                           reads=[R_ACC[a][qi], R_rec, R_SZ[zi]], writes=[R_og[o_]])
                        op("pe", lambda o_=o_, qi=qi: pe.transpose(
                            out=TPB[:, qi * 128:(qi + 1) * 128], in_=og[o_][:], identity=ident[:]),
                           reads=[R_og[o_], R_const], writes=[R_TPB])
                    op("dve", lambda a=a: dve.tensor_copy(out=aT[a][:], in_=TPB[:, 0:512]),
                       reads=[R_TPB], writes=[R_aT[a]])
                    op("pool", lambda a=a, g=g: pool.dma_start(
                        out=gh_t[hh].ap()[:, g * 512:(g + 1) * 512], in_=aT[a][:]),
                       reads=[R_aT[a]], writes=[R_gh[hh]], dma=R_aT[a])

                prev = None
                pre_at = None
                if hh == 0:
                    gpre = max(1, (NG * 5) // 8)
                    pre_at = next((ix for ix, it in enumerate(items) if it[0] >= gpre), None)
                for idx in range(len(items) + 1):
                    cur = None
                    if pre_at is not None and idx == pre_at:
                        load_head(1, extra=[R_gco] if "C" in phases else [])
                    if idx < len(items):
                        cur = (items[idx], qk_exp(items[idx]))
                    if prev is not None:
                        pv(*prev)
                        g_prev = prev[0][0]
                        if prev[0][1] == 2 * g_prev + 1:
                            finalize(g_prev)
                    prev = cur

                if "C" in phases:
                    op("pool", lambda: pool.collective_compute(
                        "AllGather", ALU.bypass, replica_groups=[list(range(NCORES))],
                        ins=[gh_t[hh].ap()], outs=[gho_t[hh].ap()]),
                       reads=[R_gh[hh]], writes=[R_gho[hh]])
            T.barrier()

        if debug:
            R_dbg = T.res("dbg", dma=True)
            op("sp", lambda: sp.dma_start(out=dbg_gci, in_=gci_t.ap()), reads=[R_gci], writes=[R_dbg], dma=R_dbg)
            for h in range(2):
                if "B" not in phases:
                    continue
                op("sp", lambda h=h: sp.dma_start(out=dbg_gh[h], in_=gh_t[h].ap()), reads=[R_gh[h]],
                   writes=[R_dbg], dma=R_dbg)
            op("sp", lambda: sp.nop(), reads=[R_dbg])
            T.barrier()

        if "D" in phases:
          with ExitStack() as st:
            wob = sb(st, "wob", [128, KC, 512], BF16)
            MX = [sb(st, f"MX{i}", [128, KC, CH], BF16) for i in range(2)]
            H = sb(st, "H", [128, 64, 512], F32)
            XT = [sb(st, f"XT{i}", [128, 512], F32) for i in range(2)]
            junk = sb(st, "junk", [128, 512], BF16)
            ssq = sb(st, "ssq", [128, 64], F32)
            ssr = sb(st, "ssr", [128, 64], F32)
            rs = sb(st, "rs", [128, 64], F32)
            nh64 = sb(st, "nh64", [128, 64], F32)
            fgs = sb(st, "fgs", [128, 512], F32)
            OP = [ps(st, f"OP{i}", [128, 512], F32) for i in range(4)]

            R_wob = [T.res(f"wob{k}") for k in range(8)]
            R_wst = [T.res(f"wst{i}", dma=True) for i in range(3)]
            wst = [H[:, 48 + 4 * i:52 + 4 * i, :] for i in range(3)]
            R_MX = [[T.res(f"MX{i}_{p}", dma=True) for p in range(3)] for i in range(2)]
            R_H = [T.res(f"H{t}") for t in range(64)]
            R_XT = [T.res(f"XT{i}", dma=True) for i in range(2)]
            R_junk = T.res("junk")
            R_ssq = T.res("ssq", dma=True)
            R_ssr = T.res("ssr", dma=True)
            R_rs = T.res("rs")
            R_nh64 = T.res("nh64")
            R_fgs = T.res("fgs", dma=True)
            R_OP = [T.res(f"OP{i}", excl=True) for i in range(4)]
            R_ari = T.res("ari")
            R_aro = T.res("aro")
            R_Hst = [T.res(f"Hst{i}", dma=True) for i in range(8)]

            op("sp", lambda: sp.dma_start(out=fgs[:], in_=fgs_d), writes=[R_fgs], dma=R_fgs)
            op("dve", lambda: dve.memset(ssq[:], 0.0), writes=[R_ssq])
            wo_r = w_out.rearrange("(k p) n -> p k n", p=128)
            for k8 in range(8):
                s_ = k8 % 3
                hres = [R_H[48 + 4 * s_ + i] for i in range(4)]
                op("sp", lambda k8=k8, s_=s_: sp.dma_start(out=wst[s_], in_=wo_r[:, k8 * 4:(k8 + 1) * 4, :]),
                   writes=[R_wst[s_]] + hres, dma=R_wst[s_])
                if k8 % 2 == 0:
                    op("dve", lambda k8=k8, s_=s_: dve.tensor_copy(
                        out=wob[:, k8 * 4:(k8 + 1) * 4, :], in_=wst[s_]),
                       reads=[R_wst[s_]] + hres, writes=[R_wob[k8]])
                else:
                    op("act", lambda k8=k8, s_=s_: act.activation(
                        out=wob[:, k8 * 4:(k8 + 1) * 4, :], in_=wst[s_], func=AF.Copy),
                       reads=[R_wst[s_]] + hres, writes=[R_wob[k8]])

            gc_r = gco_t.ap().rearrange("(k p) s -> p k s", p=128)
            gh_r = [gho_t[h].ap().rearrange("(k p) s -> p k s", p=128) for h in range(2)]

            def load_mx(tg):
                b = tg % 2
                sl = slice(tg * CH, (tg + 1) * CH)
                op("sp", lambda: sp.dma_start(out=MX[b][:, 0:16, :], in_=gc_r[:, :, sl]),
                   reads=[R_gco], writes=[R_MX[b][0]], dma=R_MX[b][0])
                for h in range(2):
                    op("sp", lambda h=h: sp.dma_start(out=MX[b][:, 16 + 8 * h:24 + 8 * h, :], in_=gh_r[h][:, :, sl]),
                       reads=[R_gho[h]], writes=[R_MX[b][1 + h]], dma=R_MX[b][1 + h])

            xctr = [0]
            load_mx(0)
            for tg in range(32):
                b = tg % 2
                if tg + 1 < 32:
                    load_mx(tg + 1)
                for tt in range(2):
                    t = 2 * tg + tt
                    bi = t % 4
                    xi = xctr[0] % 2
                    xctr[0] += 1
                    op("sp", lambda t=t, xi=xi: sp.dma_start(out=XT[xi][:], in_=x_cs[t * 128:(t + 1) * 128, :]),
                       writes=[R_XT[xi]], dma=R_XT[xi])
                    for k in range(KC):
                        part = 0 if k < 16 else (1 if k < 24 else 2)
                        op("pe", lambda k=k, tt=tt, bi=bi, b=b: pe.matmul(
                            out=OP[bi][:, :], lhsT=MX[b][:, k, tt * 128:(tt + 1) * 128], rhs=wob[:, k, :],
                            start=(k == 0), stop=(k == KC - 1)),
                           reads=[R_MX[b][part], R_wob[k // 4]], writes=[R_OP[bi]], inc=(k == KC - 1))
                    op("dve", lambda t=t, bi=bi, xi=xi: dve.tensor_tensor(
                        out=H[:, t, :], in0=OP[bi][:, :], in1=XT[xi][:], op=ALU.add),
                       reads=[R_OP[bi], R_XT[xi]], writes=[R_H[t]])
                    op("act", lambda t=t: act.activation(
                        out=junk[:], in_=H[:, t, :], func=AF.Square, accum_out=ssq[:, t:t + 1]),
                       reads=[R_H[t]], writes=[R_junk, R_ssq])
            op("pool", lambda: pool.dma_start(out=ari_t.ap(), in_=ssq[:]), reads=[R_ssq], writes=[R_ari], dma=R_ssq)
            op("pool", lambda: pool.collective_compute(
                "AllReduce", ALU.add, replica_groups=[list(range(NCORES))],
                ins=[ari_t.ap()], outs=[aro_t.ap()]), reads=[R_ari], writes=[R_aro])
            op("sp", lambda: sp.dma_start(out=ssr[:], in_=aro_t.ap()), reads=[R_aro], writes=[R_ssr], dma=R_ssr)
            op("act", lambda: act.activation(out=ssr[:], in_=ssr[:], func=AF.Sqrt, bias=EPS, scale=1.0 / DM),
               reads=[R_ssr], writes=[R_ssr])
            op("dve", lambda: dve.reciprocal(out=rs[:], in_=ssr[:]), reads=[R_ssr], writes=[R_rs])
            o_r = out.rearrange("(t p) n -> p t n", p=128)
            for t8 in range(8):
                for tt in range(8):
                    t = t8 * 8 + tt
                    op("dve", lambda t=t: dve.scalar_tensor_tensor(
                        out=H[:, t, :], in0=H[:, t, :], scalar=rs[:, t:t + 1], in1=fgs[:],
                        op0=ALU.mult, op1=ALU.mult),
                       reads=[R_H[t], R_rs, R_fgs], writes=[R_H[t]])
                op("sp", lambda t8=t8: sp.dma_start(out=o_r[:, t8 * 8:(t8 + 1) * 8, :], in_=H[:, t8 * 8:(t8 + 1) * 8, :]),
                   reads=[R_H[t8 * 8 + i] for i in range(8)], writes=[R_out], dma=R_Hst[t8])
            T.barrier()
        T.barrier(force=True)
    return nc


def _rel_bucket_np(d):
    n = np.maximum(d, 0)
    nf = np.maximum(n, 1).astype(np.float32)
    large = 16 + (np.log(nf / np.float32(16)) / np.float32(math.log(128 / 16)) * np.float32(16)).astype(np.int32)
    large = np.minimum(large, 31)
    return np.where(n < 16, n, large)


def _host_inputs(x, norm_gain, w_in, conv_w, w_out, rel_bias, final_gain):
    x2 = np.ascontiguousarray(np.asarray(x, dtype=np.float32).reshape(S, DM))
    w_in = np.asarray(w_in, dtype=np.float32)[0]
    w_out = np.asarray(w_out, dtype=np.float32)[0]
    conv_w = np.asarray(conv_w, dtype=np.float32)[0]
    g = np.asarray(norm_gain, dtype=np.float32)[0]
    rel_bias = np.asarray(rel_bias, dtype=np.float32)
    fg = np.asarray(final_gain, dtype=np.float32)
    gcol = np.ascontiguousarray(g.reshape(KC, 128).T)
    ii = np.arange(128)
    d_prev = ii[None, :] + 128 - ii[:, None]
    d_diag = ii[None, :] - ii[:, None]
    bk_prev = _rel_bucket_np(d_prev).astype(np.float32)
    bk_diag = _rel_bucket_np(d_diag).astype(np.float32)
    cm_diag = np.where(d_diag < 0, np.float32(NEG), np.float32(0.0)).astype(np.float32)
    ident = np.eye(128, dtype=np.float32).astype(ml_dtypes.bfloat16)
    rows = []
    for r in range(NCORES):
        rows.append(np.arange(2048 + r * 256, 2048 + (r + 1) * 256))
    for h in range(2):
        for r in range(NCORES):
            rows.append(np.arange((2 * r + h) * 128, (2 * r + h + 1) * 128))
    rows = np.concatenate(rows)
    w_out_p = w_out[rows, :]
    in_maps = []
    for c in range(NCORES):
        cs = slice(c * 256, (c + 1) * 256)
        segs = [0, 2048, 8192, 12288, 10240, 14336, 4096, 6144]
        wc = np.concatenate([w_in[:, o + c * 256:o + (c + 1) * 256] for o in segs], axis=1)
        cw = conv_w[:, cs]
        convw = np.ascontiguousarray(cw.reshape(3, 2, 128).transpose(2, 1, 0).reshape(128, 6))
        relb = np.ascontiguousarray(np.broadcast_to(
            rel_bias[:, 2 * c:2 * c + 2].T.reshape(1, 64), (128, 64)))
        osl = slice(c * 512, (c + 1) * 512)
        in_maps.append({
            "x": x2,
            "x_cs": np.ascontiguousarray(x2[:, osl]),
            "w_in": np.ascontiguousarray(wc),
            "w_out": np.ascontiguousarray(w_out_p[:, osl]),
            "gcol": gcol,
            "convw": convw,
            "relb": relb,
            "fgs": np.ascontiguousarray(np.broadcast_to(fg[osl].reshape(1, 512), (128, 512))),
            "bk_prev": bk_prev, "bk_diag": bk_diag, "cm_diag": cm_diag,
            "ident": ident,
        })
    return in_maps


_NC_CACHE = {}


def kernel(x, norm_gain, w_in, conv_w, w_out, rel_bias, final_gain):
    in_maps = _host_inputs(x, norm_gain, w_in, conv_w, w_out, rel_bias, final_gain)
    if "nc" not in _NC_CACHE:
        _NC_CACHE["nc"] = build_nc()
    res = run_bass_kernel_spmd(_NC_CACHE["nc"], in_maps, core_ids=list(range(NCORES)))
    outs = [np.asarray(res.results[c]["out"], dtype=np.float32) for c in range(NCORES)]
    full = np.concatenate(outs, axis=1).reshape(1, S, DM)
    return full
```

```python
import math
from contextlib import ExitStack

import numpy as np
import ml_dtypes

import concourse.bass as bass
import concourse.mybir as mybir
from concourse.bass_utils import run_bass_kernel_spmd

F32 = mybir.dt.float32
BF16 = mybir.dt.bfloat16
ALU = mybir.AluOpType
AF = mybir.ActivationFunctionType

NCORES = 8
S = 8192
DM = 4096
KC = DM // 128
CH = 256
NCHUNK = S // CH
NBLK = 32
TOK_D = S // NCORES
EPS = 1e-6
SCALE = 128.0 ** -0.5
NEG = -30000.0


class Res:
    __slots__ = ("name", "w", "r", "dsem", "dkey", "dcnt", "excl")

    def __init__(self, name):
        self.name = name
        self.excl = False
        self.w = None
        self.r = {}
        self.dsem = None
        self.dkey = None
        self.dcnt = 0


class Eng:
    def __init__(self, e, sem, key, sync_self):
        self.e = e
        self.sem = sem
        self.key = key
        self.n = 0
        self.waited = {}
        self.sync_self = sync_self


class Trk:
    def __init__(self, nc, es, nsem=90):
        self.nc = nc
        self.sems = [es.enter_context(nc.semaphore(f"sm{i}")) for i in range(nsem)]
        self.next = 0
        self.engs = {}
        for name, e, ss in (("pe", nc.tensor, False), ("act", nc.scalar, True),
                            ("dve", nc.vector, True), ("pool", nc.gpsimd, True),
                            ("sp", nc.sync, True)):
            s, k = self.newsem()
            self.engs[name] = Eng(e, s, k, ss)
        self.dres = []
        self.limit = None
        self.nops = 0
        self.log = []

    def newsem(self):
        s = self.sems[self.next]
        self.next += 1
        return s, self.next

    def res(self, name, dma=False, excl=False):
        r = Res(name)
        r.excl = excl
        if dma:
            r.dsem, r.dkey = self.newsem()
            self.dres.append(r)
        return r

    def op(self, eng, fn, reads=(), writes=(), inc=True, dma=None):
        self.nops += 1
        if self.limit is not None and self.nops > self.limit:
            return None
        E = self.engs[eng]
        ex = [r for r in reads if r.excl]
        if ex:
            reads = [r for r in reads if not r.excl]
            writes = list(writes) + ex
        deps = {}

        def add(k, s, v):
            if k not in deps or deps[k][1] < v:
                deps[k] = (s, v)

        for r in reads:
            if r.w is not None:
                add(*r.w)
        for w in writes:
            if w.w is not None:
                add(*w.w)
            for k, (s, v) in w.r.items():
                add(k, s, v)
        for k, (s, v) in deps.items():
            if k == E.key and not E.sync_self:
                continue
            if E.waited.get(k, 0) >= v:
                continue
            E.e.wait_ge(s, v)
            E.waited[k] = v
        ins = fn()
        if dma is not None:
            dma.dcnt += 16
            ins.then_inc(dma.dsem, 16)
            ev = (dma.dkey, dma.dsem, dma.dcnt)
        elif inc:
            E.n += 1
            ins.then_inc(E.sem, 1)
            ev = (E.key, E.sem, E.n)
        else:
            ev = (E.key, E.sem, E.n + 1)
        for r in reads:
            k, s, v = ev
            if k not in r.r or r.r[k][1] < v:
                r.r[k] = (s, v)
        for w in writes:
            w.w = ev
            w.r = {}
        return ins

    def barrier(self, force=False):
        if self.limit is not None and self.nops > self.limit and not force:
            return
        for E in self.engs.values():
            for F in self.engs.values():
                if F is E or F.n == 0:
                    continue
                if E.waited.get(F.key, 0) < F.n:
                    E.e.wait_ge(F.sem, F.n)
                    E.waited[F.key] = F.n
            for r in self.dres:
                if r.dcnt and E.waited.get(r.dkey, 0) < r.dcnt:
                    E.e.wait_ge(r.dsem, r.dcnt)
                    E.waited[r.dkey] = r.dcnt


def build_nc(debug=False, phases="ABCD", ntok=S, limit=None):
    nc = bass.Bass("TRN2", target_bir_lowering=False)
    S = ntok
    NCHUNK = S // CH
    NT = S // 128
    NG = S // 512

    def dram_in(name, shape, dt):
        return nc.dram_tensor(name, shape, dt, kind="ExternalInput").ap()

    def dram_scr(name, shape, dt, dbg=True):
        if debug and dbg:
            return nc.dram_tensor(name, shape, dt, kind="ExternalOutput").ap()
        return nc.dram_tensor(name, shape, dt).ap()

    x = dram_in("x", [S, DM], F32)
    w_in = dram_in("w_in", [DM, 2048], F32)
    w_out = dram_in("w_out", [DM, 512], F32)
    x_cs = dram_in("x_cs", [S, 512], F32)
    gcol_d = dram_in("gcol", [128, KC], F32)
    convw_d = dram_in("convw", [128, 6], F32)
    relb_d = dram_in("relb", [128, 64], F32)
    fgs_d = dram_in("fgs", [128, 512], F32)
    bkp_d = dram_in("bk_prev", [128, 128], F32)
    bkd_d = dram_in("bk_diag", [128, 128], F32)
    cmd_d = dram_in("cm_diag", [128, 128], F32)
    ident_d = dram_in("ident", [128, 128], BF16)
    identf_d = dram_in("identf", [128, 128], F32)
    out = nc.dram_tensor("out", [S, 512], F32, kind="ExternalOutput").ap()
    ari_t = nc.dram_tensor("ar_in", [128, 64], F32)
    aro_t = nc.dram_tensor("ar_out", [128, 64], F32)

    qT_d = dram_scr("qT_s", [2, 128, S], BF16)
    kT_d = dram_scr("kT_s", [2, 128, S], BF16)
    v_d = dram_scr("v_s", [S, 256], BF16)
    sza_d = dram_scr("sza_s", [S, 256], F32)
    mask_d = dram_scr("mask_s", [S, 64], F32)
    gci_t = nc.dram_tensor("gc_in", [256, S], BF16)
    NPART = [1, 2]
    gh_t = [[nc.dram_tensor(f"gh{h}_{p}_in", [128, S // NPART[h]], BF16) for p in range(NPART[h])]
            for h in range(2)]
    gco_t = nc.dram_tensor("gc_out", [NCORES * 256, S], BF16)
    gho_t = [[nc.dram_tensor(f"gh{h}_{p}_out", [NCORES * 128, S // NPART[h]], BF16)
              for p in range(NPART[h])] for h in range(2)]
    if debug:
        dbg_gci = nc.dram_tensor("dbg_gci", [256, S], BF16, kind="ExternalOutput").ap()
        dbg_gh = [nc.dram_tensor(f"dbg_gh{h}", [128, S], BF16, kind="ExternalOutput").ap()
                  for h in range(2)]

    with ExitStack() as es:
        T = Trk(nc, es)
        T.limit = limit
        nc._trk = T
        op = T.op
        pe, act, dve, pool, sp = nc.tensor, nc.scalar, nc.vector, nc.gpsimd, nc.sync

        def sb(st, name, shape, dt):
            return st.enter_context(nc.sbuf_tensor("s_" + name, shape, dt))

        def ps(st, name, shape, dt):
            return st.enter_context(nc.psum_tensor("p_" + name, shape, dt))

        R_qT = [T.res(f"qT{h}") for h in range(2)]
        R_kT = [T.res(f"kT{h}") for h in range(2)]
        R_v = T.res("v_d")
        R_sza = T.res("sza_d")
        R_mask = T.res("mask_d")
        R_gci = T.res("gci")
        R_gh = [[T.res(f"gh{h}_{p}") for p in range(NPART[h])] for h in range(2)]
        R_gco = T.res("gco")
        R_gho = [[T.res(f"gho{h}_{p}") for p in range(NPART[h])] for h in range(2)]
        R_out = T.res("out")

        ident = sb(es, "ident", [128, 128], BF16)
        gcol = sb(es, "gcol", [128, KC], F32)
        convw = sb(es, "convw", [128, 6], F32)
        relb = sb(es, "relb", [128, 64], F32)
        neghalf = sb(es, "neghalf", [128, 1], F32)
        R_const = T.res("const", dma=True)
        R_nh = T.res("neghalf")
        for dst, src in ((ident, ident_d), (gcol, gcol_d), (convw, convw_d), (relb, relb_d)):
            op("sp", lambda dst=dst, src=src: sp.dma_start(out=dst[:], in_=src), writes=[R_const],
               dma=R_const)

        if "A" in phases:
          with ExitStack() as st:
            Wb = sb(st, "Wb", [128, KC, 2048], BF16)
            xh = [sb(st, f"xh{i}", [128, 2048], F32) for i in range(3)]
            xs = sb(st, "xs", [128, DM], BF16)
            uT = [sb(st, f"uT{i}", [128, KC, CH], BF16) for i in range(2)]
            ss = [sb(st, f"ss{i}", [128, 2], F32) for i in range(2)]
            ms = [sb(st, f"ms{i}", [128, 1], F32) for i in range(2)]
            rstd = [sb(st, f"rstd{i}", [128, 1], F32) for i in range(2)]
            qk_sb = sb(st, "qk_sb", [128, 512], BF16)
            QK = sb(st, "QK", [128, 4, CH], BF16)
            ksum4 = sb(st, "ksum4", [128, 4], F32)
            kmT = [sb(st, f"kmT{i}", [128, NBLK], BF16) for i in range(2)]
            inv = sb(st, "inv", [128, 2], BF16)
            identf = sb(st, "identf", [128, 128], F32)
            vs = [sb(st, f"vs{i}", [128, 256], BF16) for i in range(2)]
            gsb = [sb(st, f"gsb{i}", [128, NBLK], F32) for i in range(4)]
            top8 = sb(st, "top8", [128, 8], F32)
            thr = sb(st, "thr", [128, 1], F32)
            msk = [sb(st, f"msk{i}", [128, 64], F32) for i in range(2)]
            Csb = sb(st, "Csb", [128, 256], F32)
            tmf = sb(st, "tmf", [128, 256], F32)
            szc = sb(st, "szc", [128, 256], F32)
            t1 = sb(st, "t1", [128, 2, CH + 2], F32)
            yy = sb(st, "yy", [128, CH], F32)
            cv = sb(st, "cv", [128, CH], BF16)
            tp = [ps(st, f"tp{i}", [128, 8, 128], BF16) for i in range(2)]
            mm = [ps(st, f"mm{i}", [128, 512], F32) for i in range(3)]
            tqk = ps(st, "tqk", [128, 4, 128], BF16)
            tcv = ps(st, "tcv", [128, 512], F32)
            gTp = ps(st, "gTp", [128, 2, CH], F32)

            R_W = [T.res(f"W{k}") for k in range(KC)]
            R_xh = [T.res(f"xh{i}", dma=True) for i in range(3)]
            R_xs = [T.res("xs0"), T.res("xs1")]
            R_uT = [[T.res(f"uT{i}_{g}") for g in range(4)] for i in range(2)]
            R_ss = [T.res(f"ss{i}") for i in range(2)]
            R_ms = [T.res(f"ms{i}") for i in range(2)]
            R_rstd = [T.res(f"rstd{i}") for i in range(2)]
            R_qksb = T.res("qk_sb")
            R_QK = T.res("QK", dma=True)
            R_ksum4 = T.res("ksum4")
            R_kmT = [T.res(f"kmT{i}") for i in range(2)]
            R_inv = T.res("inv")
            R_identf = T.res("identf", dma=True)
            R_vs = [T.res(f"vs{i}", dma=True) for i in range(2)]
            R_gsb = [T.res(f"gsb{i}") for i in range(4)]
            R_top8 = T.res("top8")
            R_thr = T.res("thr")
            R_msk = [T.res(f"msk{i}", dma=True) for i in range(2)]
            R_Csb = T.res("Csb")
            R_tmf = T.res("tmf")
            R_szc = T.res("szc", dma=True)
            R_t1 = T.res("t1")
            R_yy = T.res("yy")
            R_cv = T.res("cv", dma=True)
            R_tp = [T.res(f"tp{i}", excl=True) for i in range(2)]
            R_mm = [T.res(f"mm{i}", excl=True) for i in range(3)]
            R_tqk = T.res("tqk", excl=True)
            R_tcv = T.res("tcv", excl=True)
            R_gTp = T.res("gTp", excl=True)

            for i in range(2):
                op("dve", lambda i=i: dve.memset(kmT[i][:], 0.0), writes=[R_kmT[i]])
            op("dve", lambda: dve.memset(t1[:], 0.0), writes=[R_t1])
            op("dve", lambda: dve.memset(inv[:], 1.0 / CH), writes=[R_inv])
            op("sp", lambda: sp.dma_start(out=identf[:], in_=identf_d), writes=[R_identf], dma=R_identf)
            for i in range(4):
                op("dve", lambda i=i: dve.memset(gsb[i][:], -1e30), writes=[R_gsb[i]])

            for k in range(KC):
                s_ = k % 3
                op("sp", lambda k=k, s_=s_: sp.dma_start(out=xh[s_][:], in_=w_in[k * 128:(k + 1) * 128, :]),
                   writes=[R_xh[s_]], dma=R_xh[s_])
                if k % 2 == 0:
                    op("dve", lambda k=k, s_=s_: dve.tensor_scalar(
                        out=Wb[:, k, :], in0=xh[s_][:], scalar1=gcol[:, k:k + 1], scalar2=None,
                        op0=ALU.mult), reads=[R_xh[s_], R_const], writes=[R_W[k]])
                else:
                    op("act", lambda k=k, s_=s_: act.activation(
                        out=Wb[:, k, :], in_=xh[s_][:], func=AF.Copy, scale=gcol[:, k:k + 1]),
                       reads=[R_xh[s_], R_const], writes=[R_W[k]])

            slot_ctr = [0]
            fm_ctr = [0]

            prep_slots = {}

            def prep_act(c, tt):
                i = 2 * c + tt
                b = i % 2
                slots = []
                for h in range(2):
                    s_ = slot_ctr[0] % 3
                    slot_ctr[0] += 1
                    slots.append(s_)
                    op("sp", lambda i=i, h=h, s_=s_: sp.dma_start(
                        out=xh[s_][:], in_=x[i * 128:(i + 1) * 128, h * 2048:(h + 1) * 2048]),
                       writes=[R_xh[s_]], dma=R_xh[s_])
                    op("act", lambda h=h, s_=s_, b=b: act.activation(
                        out=xs[:, h * 2048:(h + 1) * 2048], in_=xh[s_][:], func=AF.Square,
                        accum_out=ss[b][:, h:h + 1]),
                       reads=[R_xh[s_]], writes=[R_xs[h], R_ss[b]])
                op("dve", lambda b=b: dve.tensor_tensor(
                    out=ms[b][:], in0=ss[b][:, 0:1], in1=ss[b][:, 1:2], op=ALU.add),
                   reads=[R_ss[b]], writes=[R_ms[b]])
                op("act", lambda b=b: act.activation(
                    out=ms[b][:], in_=ms[b][:], func=AF.Sqrt, bias=EPS, scale=1.0 / DM),
                   reads=[R_ms[b]], writes=[R_ms[b]])
                op("dve", lambda b=b: dve.reciprocal(out=rstd[b][:], in_=ms[b][:]),
                   reads=[R_ms[b]], writes=[R_rstd[b]])
                for h in range(2):
                    s_ = slots[h]
                    op("act", lambda h=h, s_=s_, b=b: act.activation(
                        out=xs[:, h * 2048:(h + 1) * 2048], in_=xh[s_][:], func=AF.Copy,
                        scale=rstd[b][:, 0:1]),
                       reads=[R_xh[s_], R_rstd[b]], writes=[R_xs[h]])

            def prep_pe(c, tt):
                u = c % 2
                for kg in range(4):
                    pb = kg % 2
                    for kk in range(8):
                        k = kg * 8 + kk
                        op("pe", lambda k=k, kk=kk, pb=pb: pe.transpose(
                            out=tp[pb][:, kk, :], in_=xs[:, k * 128:(k + 1) * 128],
                            identity=ident[:]),
                           reads=[R_xs[k // 16], R_const], writes=[R_tp[pb]], inc=(kk == 7))
                    op("dve", lambda kg=kg, pb=pb, u=u, tt=tt: dve.tensor_copy(
                        out=uT[u][:, kg * 8:(kg + 1) * 8, tt * 128:(tt + 1) * 128],
                        in_=tp[pb][:]),
                       reads=[R_tp[pb]], writes=[R_uT[u][kg]])

            def inproj(c, hooks):
                u = c % 2
                deferred = []
                for f_ in hooks.get(-1, []):
                    f_()

                def mm_group(tt, n):
                    bi = fm_ctr[0] % 3
                    fm_ctr[0] += 1
                    for k in range(KC):
                        op("pe", lambda k=k, bi=bi: pe.matmul(
                            out=mm[bi][:, :], lhsT=uT[u][:, k, tt * 128:(tt + 1) * 128],
                            rhs=Wb[:, k, n * 512:(n + 1) * 512], start=(k == 0), stop=(k == KC - 1)),
                           reads=[R_W[k], R_uT[u][k // 8]], writes=[R_mm[bi]], inc=(k == KC - 1))
                    for f_ in hooks.get(tt * 4 + n, []):
                        f_()
                    return bi

                def flush():
                    while deferred:
                        deferred.pop(0)()

                for tt in range(2):
                    row0 = (2 * c + tt) * 128
                    tsl = slice(tt * 128, (tt + 1) * 128)
                    bi = mm_group(tt, 0)
                    flush()
                    op("dve", lambda bi=bi: dve.tensor_copy(out=qk_sb[:], in_=mm[bi][:, :]),
                       reads=[R_mm[bi]], writes=[R_qksb])

                    def d_qk(tt=tt, tsl=tsl):
                        for i in range(4):
                            op("pe", lambda i=i: pe.transpose(
                                out=tqk[:, i, :], in_=qk_sb[:, i * 128:(i + 1) * 128], identity=ident[:]),
                               reads=[R_qksb, R_const], writes=[R_tqk], inc=False)
                        for h in range(2):
                            op("pe", lambda h=h: pe.matmul(
                                out=tcv[:, 384 + tt * 2 + h:385 + tt * 2 + h],
                                lhsT=qk_sb[:, 256 + h * 128:256 + (h + 1) * 128], rhs=inv[:, 0:1],
                                start=True, stop=True),
                               reads=[R_qksb, R_inv], writes=[R_tcv], inc=(h == 1))
                        op("dve", lambda: dve.tensor_copy(out=QK[:, :, tsl], in_=tqk[:]),
                           reads=[R_tqk], writes=[R_QK])
                    deferred.append(d_qk)
                    bi = mm_group(tt, 1)
                    flush()
                    op("act", lambda bi=bi: act.activation(out=Csb[:], in_=mm[bi][:, 256:512], func=AF.Copy),
                       reads=[R_mm[bi]], writes=[R_Csb])
                    op("dve", lambda bi=bi: dve.tensor_tensor(
                        out=tmf[:], in0=mm[bi][:, 0:256], in1=Csb[:], op=ALU.mult),
                       reads=[R_mm[bi], R_Csb], writes=[R_tmf])

                    def d_t1(tsl=tsl, tt=tt):
                        for j in range(2):
                            op("pe", lambda j=j: pe.transpose(
                                out=tcv[:, j * 128:(j + 1) * 128], in_=tmf[:, j * 128:(j + 1) * 128],
                                identity=identf[:]),
                               reads=[R_tmf, R_identf], writes=[R_tcv], inc=(j == 1))
                        op("act", lambda: act.activation(
                            out=t1[:, :, 2 + tt * 128:2 + (tt + 1) * 128],
                            in_=tcv[:, 0:256].rearrange("p (j t) -> p j t", j=2), func=AF.Copy),
                           reads=[R_tcv], writes=[R_t1])
                    deferred.append(d_t1)
                    bi = mm_group(tt, 2)
                    flush()
                    op("act", lambda bi=bi: act.activation(out=szc[:], in_=mm[bi][:, 256:512], func=AF.Silu),
                       reads=[R_mm[bi]], writes=[R_szc])
                    op("dve", lambda bi=bi: dve.tensor_tensor(
                        out=tmf[:], in0=mm[bi][:, 0:256], in1=szc[:], op=ALU.mult),
                       reads=[R_mm[bi], R_szc], writes=[R_tmf])

                    def d_g(tsl=tsl):
                        for j in range(2):
                            op("pe", lambda j=j: pe.transpose(
                                out=gTp[:, j, tsl], in_=tmf[:, j * 128:(j + 1) * 128], identity=identf[:]),
                               reads=[R_tmf, R_identf], writes=[R_gTp], inc=(j == 1))
                    deferred.append(d_g)
                    bi = mm_group(tt, 3)
                    flush()
                    op("dve", lambda bi=bi, tt=tt: dve.tensor_copy(out=vs[tt][:], in_=mm[bi][:, 0:256]),
                       reads=[R_mm[bi]], writes=[R_vs[tt]])
                    op("pool", lambda tt=tt, row0=row0: pool.dma_start(
                        out=v_d[row0:row0 + 128, :], in_=vs[tt][:]),
                       reads=[R_vs[tt]], writes=[R_v], dma=R_vs[tt])
                    op("act", lambda bi=bi: act.activation(out=szc[:], in_=mm[bi][:, 256:512], func=AF.Silu),
                       reads=[R_mm[bi]], writes=[R_szc])
                    op("pool", lambda row0=row0: pool.dma_start(
                        out=sza_d[row0:row0 + 128, :], in_=szc[:]),
                       reads=[R_szc], writes=[R_sza], dma=R_szc)
                flush()
                for h in range(2):
                    op("pool", lambda h=h: pool.dma_start(
                        out=qT_d[h, :, c * CH:(c + 1) * CH], in_=QK[:, h, :]),
                       reads=[R_QK], writes=[R_qT[h]], dma=R_QK)
                    op("pool", lambda h=h: pool.dma_start(
                        out=kT_d[h, :, c * CH:(c + 1) * CH], in_=QK[:, 2 + h, :]),
                       reads=[R_QK], writes=[R_kT[h]], dma=R_QK)
                for r_ in (R_qT[0], R_qT[1], R_kT[0], R_kT[1]):
                    r_.w = (R_QK.dkey, R_QK.dsem, R_QK.dcnt)
                for h in range(2):
                    for tt in range(2):
                        gi = h * 2 + tt
                        op("pe", lambda h=h, tt=tt, gi=gi: pe.matmul(
                            out=tcv[:, 256 + gi * 32:256 + (gi + 1) * 32],
                            lhsT=QK[:, h, tt * 128:(tt + 1) * 128], rhs=kmT[h][:],
                            start=True, stop=True),
                           reads=[R_QK, R_kmT[h]], writes=[R_tcv], inc=(gi == 3))
                for tt in range(2):
                    for h in range(2):
                        gi = h * 2 + tt
                        if c >= 1:
                            op("dve", lambda gi=gi: dve.tensor_copy(
                                out=gsb[gi][:, 0:c], in_=tcv[:, 256 + gi * 32:256 + gi * 32 + c]),
                               reads=[R_tcv], writes=[R_gsb[gi]])
                        op("dve", lambda gi=gi: dve.max(out=top8[:], in_=gsb[gi][:]),
                           reads=[R_gsb[gi]], writes=[R_top8])
                        op("dve", lambda: dve.tensor_scalar(
                            out=thr[:], in0=top8[:, 2:3], scalar1=-1e29, scalar2=None, op0=ALU.max),
                           reads=[R_top8], writes=[R_thr])
                        op("dve", lambda gi=gi, h=h, tt=tt: dve.tensor_scalar(
                            out=msk[tt][:, h * 32:(h + 1) * 32], in0=gsb[gi][:],
                            scalar1=thr[:, 0:1], scalar2=None, op0=ALU.is_ge),
                           reads=[R_gsb[gi], R_thr], writes=[R_msk[tt]])
                    op("pool", lambda tt=tt: pool.dma_start(
                        out=mask_d[(2 * c + tt) * 128:(2 * c + tt + 1) * 128, :], in_=msk[tt][:]),
                       reads=[R_msk[tt]], writes=[R_mask], dma=R_msk[tt])
                op("dve", lambda: dve.tensor_copy(out=ksum4[:], in_=tcv[:, 384:388]),
                   reads=[R_tcv], writes=[R_ksum4])
                for h in range(2):
                    op("dve", lambda h=h: dve.tensor_tensor(
                        out=kmT[h][:, c:c + 1], in0=ksum4[:, h:h + 1], in1=ksum4[:, 2 + h:3 + h], op=ALU.add),
                       reads=[R_ksum4], writes=[R_kmT[h]])
                for j in range(2):
                    op("dve", lambda j=j: dve.tensor_scalar(
                        out=yy[:], in0=t1[:, j, 0:CH], scalar1=convw[:, j * 3:j * 3 + 1],
                        scalar2=None, op0=ALU.mult),
                       reads=[R_t1, R_const], writes=[R_yy])
                    for tap in (1, 2):
                        op("dve", lambda j=j, tap=tap: dve.scalar_tensor_tensor(
                            out=yy[:], in0=t1[:, j, tap:tap + CH],
                            scalar=convw[:, j * 3 + tap:j * 3 + tap + 1], in1=yy[:],
                            op0=ALU.mult, op1=ALU.add),
                           reads=[R_t1, R_const, R_yy], writes=[R_yy])
                    op("dve", lambda j=j: dve.tensor_tensor(
                        out=cv[:], in0=gTp[:, j, :], in1=yy[:], op=ALU.mult),
                       reads=[R_gTp, R_yy], writes=[R_cv])
                    op("pool", lambda j=j: pool.dma_start(
                        out=gci_t.ap()[j * 128:(j + 1) * 128, c * CH:(c + 1) * CH], in_=cv[:]),
                       reads=[R_cv], writes=[R_gci], dma=R_cv)
                op("dve", lambda: dve.tensor_copy(out=t1[:, :, 0:2], in_=t1[:, :, CH:CH + 2]),
                   reads=[R_t1], writes=[R_t1])

            for tt in range(2):
                prep_act(0, tt)
                prep_pe(0, tt)
            for c in range(NCHUNK):
                hooks = {}
                if c + 1 < NCHUNK:
                    n_ = c + 1
                    hooks = {-1: [lambda n_=n_: prep_act(n_, 0)],
                             1: [lambda n_=n_: prep_pe(n_, 0), lambda n_=n_: prep_act(n_, 1)],
                             5: [lambda n_=n_: prep_pe(n_, 1)]}
                inproj(c, hooks)
            T.barrier()


        if "B" in phases:
          with ExitStack() as st:
            KT2 = [sb(st, f"KT{i}", [128, S], BF16) for i in range(2)]
            QT2 = [sb(st, f"QT{i}", [128, S], BF16) for i in range(2)]
            VA2 = [sb(st, f"VA{i}", [128, NT, 130], BF16) for i in range(2)]
            MK2 = [sb(st, f"MK{i}", [128, NT, 32], F32) for i in range(2)]
            SZ = [sb(st, f"SZ{i}", [128, 4, 128], F32) for i in range(2)]
            NPT = 6
            PT = [sb(st, f"PT{i}", [128, 512], BF16) for i in range(NPT)]
            NSS = 4
            Ssb = [sb(st, f"Ssb{i}", [128, 128], F32) for i in range(NSS)]
            ACC = [sb(st, f"ACC{i}", [128, 4, 130], F32) for i in range(2)]
            rec = sb(st, "rec", [128, 1], F32)
            og = [sb(st, f"og{i}", [128, 128], BF16) for i in range(2)]
            aT = [sb(st, f"aT{i}", [128, 512], BF16) for i in range(2)]
            NSB = 3
            SB = [ps(st, f"SB{i}", [128, 512], F32) for i in range(NSB)]
            NOB = 3
            OB = [ps(st, f"OB{i}", [128, 512], F32) for i in range(NOB)]
            TPB = ps(st, "TPB", [128, 1024], BF16)

            R_KT2 = [T.res(f"KT{i}", dma=True) for i in range(2)]
            R_QT2 = [T.res(f"QT{i}", dma=True) for i in range(2)]
            R_VA2 = [T.res(f"VA{i}", dma=True) for i in range(2)]
            R_MK2 = [T.res(f"MK{i}", dma=True) for i in range(2)]
            R_SZ = [T.res(f"SZ{i}", dma=True) for i in range(2)]
            R_PT = [T.res(f"PT{i}") for i in range(NPT)]
            R_Ssb = [T.res(f"Ssb{i}") for i in range(NSS)]
            R_ACC = [[T.res(f"ACC{i}_{q}") for q in range(4)] for i in range(2)]
            R_rec = T.res("rec")
            R_og = [T.res(f"og{i}") for i in range(2)]
            R_aT = [T.res(f"aT{i}", dma=True) for i in range(2)]
            R_SB = [T.res(f"SB{i}", excl=True) for i in range(NSB)]
            R_OB = [T.res(f"OB{i}", excl=True) for i in range(NOB)]
            R_TPB = T.res("TPB", excl=True)
            TT = sb(st, "TT", [128, 4, 128], F32)
            R_TT = T.res("TT")
            if True:
                bk = sb(st, "bk", [128, 3, 128], F32)
                tmp = sb(st, "tt_tmp", [128, 2, 128], F32)
                R_bk = T.res("bk", dma=True)
                R_tmp = T.res("tt_tmp")
                for i, src in enumerate((bkp_d, bkd_d, cmd_d)):
                    op("sp", lambda i=i, src=src: sp.dma_start(out=bk[:, i, :], in_=src),
                       dma=R_bk)
                R_bk.w = (R_bk.dkey, R_bk.dsem, R_bk.dcnt)
                for h in range(2):
                    dstt = TT[:, h * 2:h * 2 + 2, :]
                    for b in range(32):
                        if b == 0:
                            op("dve", lambda dstt=dstt, h=h, b=b: dve.tensor_scalar(
                                out=dstt, in0=bk[:, 0:2, :], scalar1=float(b),
                                scalar2=relb[:, h * 32 + b:h * 32 + b + 1],
                                op0=ALU.is_equal, op1=ALU.mult),
                               reads=[R_bk, R_const], writes=[R_TT])
                        else:
                            op("dve", lambda h=h, b=b: dve.tensor_scalar(
                                out=tmp[:], in0=bk[:, 0:2, :], scalar1=float(b),
                                scalar2=relb[:, h * 32 + b:h * 32 + b + 1],
                                op0=ALU.is_equal, op1=ALU.mult),
                               reads=[R_bk, R_const], writes=[R_tmp])
                            op("dve", lambda dstt=dstt: dve.tensor_tensor(
                                out=dstt, in0=dstt, in1=tmp[:], op=ALU.add),
                               reads=[R_tmp, R_TT], writes=[R_TT])
                    op("dve", lambda h=h: dve.tensor_tensor(
                        out=TT[:, h * 2 + 1, :], in0=TT[:, h * 2 + 1, :], in1=bk[:, 2, :], op=ALU.add),
                       reads=[R_bk, R_TT], writes=[R_TT])

            for i in range(2):
                op("dve", lambda i=i: dve.memset(VA2[i][:, :, 128:130], 1.0), writes=[R_VA2[i]])
            v_r = v_d.rearrange("(t p) d -> p t d", p=128)
            m_r = mask_d.rearrange("(t p) m -> p t m", p=128)
            z_r = sza_d.rearrange("(t p) d -> p t d", p=128)

            ctr = {"pt": 0, "ss": 0, "sb": 0, "ob": 0, "og": 0, "sz": 0}

            def load_head(hh, extra=()):
                KT, QT, VA, MK = KT2[hh], QT2[hh], VA2[hh], MK2[hh]
                R_KT, R_QT, R_VA, R_MK = R_KT2[hh], R_QT2[hh], R_VA2[hh], R_MK2[hh]
                extra = list(extra)
                for q4 in range(4):
                    sl = slice(q4 * (S // 4), (q4 + 1) * (S // 4))
                    op("sp", lambda sl=sl: sp.dma_start(out=KT[:, sl], in_=kT_d[hh, :, sl]),
                       reads=[R_kT[hh]] + extra, writes=[R_KT], dma=R_KT)
                    op("sp", lambda sl=sl: sp.dma_start(out=QT[:, sl], in_=qT_d[hh, :, sl]),
                       reads=[R_qT[hh]], writes=[R_QT], dma=R_QT)
                for t8 in range(8):
                    op("sp", lambda t8=t8: sp.dma_start(
                        out=VA[:, t8 * (NT // 8):(t8 + 1) * (NT // 8), 0:128],
                        in_=v_r[:, t8 * (NT // 8):(t8 + 1) * (NT // 8), hh * 128:(hh + 1) * 128]),
                       reads=[R_v], writes=[R_VA], dma=R_VA)
                for t4 in range(4):
                    op("sp", lambda t4=t4: sp.dma_start(
                        out=MK[:, t4 * (NT // 4):(t4 + 1) * (NT // 4), :],
                        in_=m_r[:, t4 * (NT // 4):(t4 + 1) * (NT // 4), hh * 32:(hh + 1) * 32]),
                       reads=[R_mask], writes=[R_MK], dma=R_MK)
                for r_ in (R_KT, R_QT, R_VA, R_MK):
                    r_.w = (r_.dkey, r_.dsem, r_.dcnt)

            load_head(0)
            if "C" in phases:
                op("pool", lambda: pool.collective_compute(
                    "AllGather", ALU.bypass, replica_groups=[list(range(NCORES))],
                    ins=[gci_t.ap()], outs=[gco_t.ap()]),
                   reads=[R_gci, R_KT2[0], R_QT2[0], R_VA2[0], R_MK2[0]], writes=[R_gco])

            for hh in range(2):
                KT, QT, VA, MK = KT2[hh], QT2[hh], VA2[hh], MK2[hh]
                R_KT, R_QT, R_VA, R_MK = R_KT2[hh], R_QT2[hh], R_VA2[hh], R_MK2[hh]
                c31 = relb[:, hh * 32 + 31:hh * 32 + 32]
                TTp = TT[:, hh * 2 + 0, :]
                TTd = TT[:, hh * 2 + 1, :]

                items = []
                for g in range(NG):
                    for j in range(2 * g + 2):
                        kts = []
                        for half in range(2):
                            kt = 2 * j + half
                            segs = []
                            for qi in range(4):
                                qt = 4 * g + qi
                                if kt > qt:
                                    continue
                                if kt == qt:
                                    segs.append((qi, "diag"))
                                elif kt == qt - 1:
                                    segs.append((qi, "prev"))
                                else:
                                    segs.append((qi, "far"))
                            if segs:
                                kts.append((kt, segs))
                        items.append((g, j, kts))

                state = {}

                def qk_exp(item):
                    g, j, kts = item
                    outl = []
                    for kt, segs in kts:
                        q0 = segs[0][0]
                        ncol = (4 - q0) * 128
                        bi = ctr["sb"] % NSB
                        ctr["sb"] += 1
                        pi = ctr["pt"] % NPT
                        ctr["pt"] += 1
                        op("pe", lambda kt=kt, q0=q0, ncol=ncol, bi=bi, g=g: pe.matmul(
                            out=SB[bi][:, q0 * 128:512], lhsT=KT[:, kt * 128:(kt + 1) * 128],
                            rhs=QT[:, g * 512 + q0 * 128:(g + 1) * 512], start=True, stop=True),
                           reads=[R_KT, R_QT], writes=[R_SB[bi]])
                        far0 = None
                        for qi, kind in segs:
                            if kind == "far":
                                if far0 is None:
                                    far0 = qi
                                continue
                            si = ctr["ss"] % NSS
                            ctr["ss"] += 1
                            tt_ = TTd if kind == "diag" else TTp
                            cs = slice(qi * 128, (qi + 1) * 128)
                            op("dve", lambda bi=bi, si=si, tt_=tt_, cs=cs: dve.scalar_tensor_tensor(
                                out=Ssb[si][:], in0=SB[bi][:, cs], scalar=SCALE, in1=tt_,
                                op0=ALU.mult, op1=ALU.add),
                               reads=[R_SB[bi], R_TT], writes=[R_Ssb[si]])
                            op("act", lambda si=si, pi=pi, cs=cs: act.activation(
                                out=PT[pi][:, cs], in_=Ssb[si][:], func=AF.Exp),
                               reads=[R_Ssb[si]], writes=[R_PT[pi]])
                        if far0 is not None:
                            cs = slice(far0 * 128, 512)
                            op("act", lambda bi=bi, pi=pi, cs=cs: act.activation(
                                out=PT[pi][:, cs], in_=SB[bi][:, cs], func=AF.Exp, bias=c31,
                                scale=SCALE),
                               reads=[R_SB[bi], R_const], writes=[R_PT[pi]])
                        outl.append((kt, pi, [s[0] for s in segs]))
                    return outl

                def pv(item, ptl):
                    g, j, kts = item
                    a = g % 2
                    for qi in range(4):
                        qt = 4 * g + qi
                        parts = [(kt, pi) for kt, pi, qis in ptl if qi in qis]
                        if not parts:
                            continue
                        oi = ctr["ob"] % NOB
                        ctr["ob"] += 1
                        for n_, (kt, pi) in enumerate(parts):
                            op("pe", lambda kt=kt, pi=pi, oi=oi, qi=qi, n_=n_, L=len(parts): pe.matmul(
                                out=OB[oi][:, 0:129], lhsT=PT[pi][:, qi * 128:(qi + 1) * 128],
                                rhs=VA[:, kt, 0:129], start=(n_ == 0), stop=(n_ == L - 1)),
                               reads=[R_PT[pi], R_VA], writes=[R_OB[oi]], inc=(n_ == len(parts) - 1))
                        own = (j == qt // 2)
                        first = (g, qi) not in state
                        state[(g, qi)] = True
                        accv = ACC[a][:, qi, 0:129]
                        mcol = MK[:, qt, j:j + 1]
                        if first and own:
                            op("dve", lambda oi=oi, accv=accv: dve.tensor_copy(out=accv, in_=OB[oi][:, 0:129]),
                               reads=[R_OB[oi]], writes=[R_ACC[a][qi]])
                        elif first:
                            op("dve", lambda oi=oi, accv=accv, mcol=mcol: dve.tensor_scalar(
                                out=accv, in0=OB[oi][:, 0:129], scalar1=mcol, scalar2=None, op0=ALU.mult),
                               reads=[R_OB[oi], R_MK], writes=[R_ACC[a][qi]])
                        elif own:
                            op("dve", lambda oi=oi, accv=accv: dve.tensor_tensor(
                                out=accv, in0=OB[oi][:, 0:129], in1=accv, op=ALU.add),
                               reads=[R_OB[oi], R_ACC[a][qi]], writes=[R_ACC[a][qi]])
                        else:
                            op("dve", lambda oi=oi, accv=accv, mcol=mcol: dve.scalar_tensor_tensor(
                                out=accv, in0=OB[oi][:, 0:129], scalar=mcol, in1=accv,
                                op0=ALU.mult, op1=ALU.add),
                               reads=[R_OB[oi], R_MK, R_ACC[a][qi]], writes=[R_ACC[a][qi]])

                def finalize(g):
                    a = g % 2
                    zi = ctr["sz"] % 2
                    ctr["sz"] += 1
                    op("sp", lambda zi=zi, g=g: sp.dma_start(
                        out=SZ[zi][:], in_=z_r[:, 4 * g:4 * g + 4, hh * 128:(hh + 1) * 128]),
                       reads=[R_sza], writes=[R_SZ[zi]], dma=R_SZ[zi])
                    for qi in range(4):
                        o_ = ctr["og"] % 2
                        ctr["og"] += 1
                        op("dve", lambda a=a, qi=qi: dve.reciprocal(out=rec[:], in_=ACC[a][:, qi, 128:129]),
                           reads=[R_ACC[a][qi]], writes=[R_rec])
                        op("dve", lambda a=a, qi=qi, o_=o_, zi=zi: dve.scalar_tensor_tensor(
                            out=og[o_][:], in0=ACC[a][:, qi, 0:128], scalar=rec[:, 0:1],
                            in1=SZ[zi][:, qi, :], op0=ALU.mult, op1=ALU.mult),
                           reads=[R_ACC[a][qi], R_rec, R_SZ[zi]], writes=[R_og[o_]])
                        op("pe", lambda o_=o_, qi=qi: pe.transpose(
                            out=TPB[:, qi * 128:(qi + 1) * 128], in_=og[o_][:], identity=ident[:]),
                           reads=[R_og[o_], R_const], writes=[R_TPB])
                    op("dve", lambda a=a: dve.tensor_copy(out=aT[a][:], in_=TPB[:, 0:512]),
                       reads=[R_TPB], writes=[R_aT[a]])
                    gpp = NG // NPART[hh]
                    part, gl = g // gpp, g % gpp
                    op("pool", lambda a=a, gl=gl, part=part: pool.dma_start(
                        out=gh_t[hh][part].ap()[:, gl * 512:(gl + 1) * 512], in_=aT[a][:]),
                       reads=[R_aT[a]], writes=[R_gh[hh][part]], dma=R_aT[a])
                    if "C" in phases and gl == gpp - 1:
                        op("pool", lambda part=part: pool.collective_compute(
                            "AllGather", ALU.bypass, replica_groups=[list(range(NCORES))],
                            ins=[gh_t[hh][part].ap()], outs=[gho_t[hh][part].ap()]),
                           reads=[R_gh[hh][part]], writes=[R_gho[hh][part]])

                prev = None
                pre_at = None
                if hh == 0:
                    gpre = max(1, (NG * 5) // 8)
                    pre_at = next((ix for ix, it in enumerate(items) if it[0] >= gpre), None)
                for idx in range(len(items) + 1):
                    cur = None
                    if pre_at is not None and idx == pre_at:
                        load_head(1, extra=[R_gco] if "C" in phases else [])
                    if idx < len(items):
                        cur = (items[idx], qk_exp(items[idx]))
                    if prev is not None:
                        pv(*prev)
                        g_prev = prev[0][0]
                        if prev[0][1] == 2 * g_prev + 1:
                            finalize(g_prev)
                    prev = cur

            T.barrier()

        if debug:
            R_dbg = T.res("dbg", dma=True)
            op("sp", lambda: sp.dma_start(out=dbg_gci, in_=gci_t.ap()), reads=[R_gci], writes=[R_dbg], dma=R_dbg)
            for h in range(2):
                if "B" not in phases:
                    continue
                for p in range(NPART[h]):
                    w_ = S // NPART[h]
                    op("sp", lambda h=h, p=p, w_=w_: sp.dma_start(
                        out=dbg_gh[h][:, p * w_:(p + 1) * w_], in_=gh_t[h][p].ap()),
                       reads=[R_gh[h][p]], writes=[R_dbg], dma=R_dbg)
            op("sp", lambda: sp.nop(), reads=[R_dbg])
            T.barrier()

        if "D" in phases:
          with ExitStack() as st:
            wob = sb(st, "wob", [128, KC, 512], BF16)
            MX = [sb(st, f"MX{i}", [128, KC, CH], BF16) for i in range(2)]
            H = sb(st, "H", [128, 64, 512], F32)
            XT = [sb(st, f"XT{i}", [128, 512], F32) for i in range(2)]
            junk = sb(st, "junk", [128, 512], BF16)
            ssq = sb(st, "ssq", [128, 64], F32)
            ssr = sb(st, "ssr", [128, 64], F32)
            rs = sb(st, "rs", [128, 64], F32)
            nh64 = sb(st, "nh64", [128, 64], F32)
            fgs = sb(st, "fgs", [128, 512], F32)
            OP = [ps(st, f"OP{i}", [128, 512], F32) for i in range(4)]

            R_wob = [T.res(f"wob{k}") for k in range(8)]
            R_wst = [T.res(f"wst{i}", dma=True) for i in range(3)]
            wst = [H[:, 48 + 4 * i:52 + 4 * i, :] for i in range(3)]
            R_MX = [[T.res(f"MX{i}_{p}", dma=True) for p in range(3)] for i in range(2)]
            R_H = [T.res(f"H{t}") for t in range(64)]
            R_XT = [T.res(f"XT{i}", dma=True) for i in range(2)]
            R_junk = T.res("junk")
            R_ssq = T.res("ssq", dma=True)
            R_ssr = T.res("ssr", dma=True)
            R_rs = T.res("rs")
            R_nh64 = T.res("nh64")
            R_fgs = T.res("fgs", dma=True)
            R_OP = [T.res(f"OP{i}", excl=True) for i in range(4)]
            R_ari = T.res("ari")
            R_aro = T.res("aro")
            R_Hst = [T.res(f"Hst{i}", dma=True) for i in range(8)]

            op("sp", lambda: sp.dma_start(out=fgs[:], in_=fgs_d), writes=[R_fgs], dma=R_fgs)
            op("dve", lambda: dve.memset(ssq[:], 0.0), writes=[R_ssq])
            wo_r = w_out.rearrange("(k p) n -> p k n", p=128)
            for k8 in range(8):
                s_ = k8 % 3
                hres = [R_H[48 + 4 * s_ + i] for i in range(4)]
                op("sp", lambda k8=k8, s_=s_: sp.dma_start(out=wst[s_], in_=wo_r[:, k8 * 4:(k8 + 1) * 4, :]),
                   writes=[R_wst[s_]] + hres, dma=R_wst[s_])
                if k8 % 2 == 0:
                    op("dve", lambda k8=k8, s_=s_: dve.tensor_copy(
                        out=wob[:, k8 * 4:(k8 + 1) * 4, :], in_=wst[s_]),
                       reads=[R_wst[s_]] + hres, writes=[R_wob[k8]])
                else:
                    op("act", lambda k8=k8, s_=s_: act.activation(
                        out=wob[:, k8 * 4:(k8 + 1) * 4, :], in_=wst[s_], func=AF.Copy),
                       reads=[R_wst[s_]] + hres, writes=[R_wob[k8]])

            gc_r = gco_t.ap().rearrange("(k p) s -> p k s", p=128)
            gh_r = [[gho_t[h][p].ap().rearrange("(k p) s -> p k s", p=128) for p in range(NPART[h])]
                    for h in range(2)]

            def load_mx(tg):
                b = tg % 2
                sl = slice(tg * CH, (tg + 1) * CH)
                op("sp", lambda: sp.dma_start(out=MX[b][:, 0:16, :], in_=gc_r[:, :, sl]),
                   reads=[R_gco], writes=[R_MX[b][0]], dma=R_MX[b][0])
                for h in range(2):
                    cpp = NCHUNK // NPART[h]
                    part, tl = tg // cpp, tg % cpp
                    slh = slice(tl * CH, (tl + 1) * CH)
                    op("sp", lambda h=h, part=part, slh=slh: sp.dma_start(
                        out=MX[b][:, 16 + 8 * h:24 + 8 * h, :], in_=gh_r[h][part][:, :, slh]),
                       reads=[R_gho[h][part]], writes=[R_MX[b][1 + h]], dma=R_MX[b][1 + h])

            xctr = [0]
            load_mx(0)
            for tg in range(32):
                b = tg % 2
                if tg + 1 < 32:
                    load_mx(tg + 1)
                for tt in range(2):
                    t = 2 * tg + tt
                    bi = t % 4
                    xi = xctr[0] % 2
                    xctr[0] += 1
                    op("sp", lambda t=t, xi=xi: sp.dma_start(out=XT[xi][:], in_=x_cs[t * 128:(t + 1) * 128, :]),
                       writes=[R_XT[xi]], dma=R_XT[xi])
                    for k in range(KC):
                        part = 0 if k < 16 else (1 if k < 24 else 2)
                        op("pe", lambda k=k, tt=tt, bi=bi, b=b: pe.matmul(
                            out=OP[bi][:, :], lhsT=MX[b][:, k, tt * 128:(tt + 1) * 128], rhs=wob[:, k, :],
                            start=(k == 0), stop=(k == KC - 1)),
                           reads=[R_MX[b][part], R_wob[k // 4]], writes=[R_OP[bi]], inc=(k == KC - 1))
                    op("dve", lambda t=t, bi=bi, xi=xi: dve.tensor_tensor(
                        out=H[:, t, :], in0=OP[bi][:, :], in1=XT[xi][:], op=ALU.add),
                       reads=[R_OP[bi], R_XT[xi]], writes=[R_H[t]])
                    op("act", lambda t=t: act.activation(
                        out=junk[:], in_=H[:, t, :], func=AF.Square, accum_out=ssq[:, t:t + 1]),
                       reads=[R_H[t]], writes=[R_junk, R_ssq])
            op("pool", lambda: pool.dma_start(out=ari_t.ap(), in_=ssq[:]), reads=[R_ssq], writes=[R_ari], dma=R_ssq)
            op("pool", lambda: pool.collective_compute(
                "AllReduce", ALU.add, replica_groups=[list(range(NCORES))],
                ins=[ari_t.ap()], outs=[aro_t.ap()]), reads=[R_ari], writes=[R_aro])
            op("sp", lambda: sp.dma_start(out=ssr[:], in_=aro_t.ap()), reads=[R_aro], writes=[R_ssr], dma=R_ssr)
            op("act", lambda: act.activation(out=ssr[:], in_=ssr[:], func=AF.Sqrt, bias=EPS, scale=1.0 / DM),
               reads=[R_ssr], writes=[R_ssr])
            op("dve", lambda: dve.reciprocal(out=rs[:], in_=ssr[:]), reads=[R_ssr], writes=[R_rs])
            o_r = out.rearrange("(t p) n -> p t n", p=128)
            for t8 in range(8):
                for tt in range(8):
                    t = t8 * 8 + tt
                    op("dve", lambda t=t: dve.scalar_tensor_tensor(
                        out=H[:, t, :], in0=H[:, t, :], scalar=rs[:, t:t + 1], in1=fgs[:],
                        op0=ALU.mult, op1=ALU.mult),
                       reads=[R_H[t], R_rs, R_fgs], writes=[R_H[t]])
                op("sp", lambda t8=t8: sp.dma_start(out=o_r[:, t8 * 8:(t8 + 1) * 8, :], in_=H[:, t8 * 8:(t8 + 1) * 8, :]),
                   reads=[R_H[t8 * 8 + i] for i in range(8)], writes=[R_out], dma=R_Hst[t8])
            T.barrier()
        T.barrier(force=True)
    return nc


def _rel_bucket_np(d):
    n = np.maximum(d, 0)
    nf = np.maximum(n, 1).astype(np.float32)
    large = 16 + (np.log(nf / np.float32(16)) / np.float32(math.log(128 / 16)) * np.float32(16)).astype(np.int32)
    large = np.minimum(large, 31)
    return np.where(n < 16, n, large)


def _host_inputs(x, norm_gain, w_in, conv_w, w_out, rel_bias, final_gain):
    x2 = np.ascontiguousarray(np.asarray(x, dtype=np.float32).reshape(S, DM))
    w_in = np.asarray(w_in, dtype=np.float32)[0]
    w_out = np.asarray(w_out, dtype=np.float32)[0]
    conv_w = np.asarray(conv_w, dtype=np.float32)[0]
    g = np.asarray(norm_gain, dtype=np.float32)[0]
    rel_bias = np.asarray(rel_bias, dtype=np.float32)
    fg = np.asarray(final_gain, dtype=np.float32)
    gcol = np.ascontiguousarray(g.reshape(KC, 128).T)
    ii = np.arange(128)
    d_prev = ii[None, :] + 128 - ii[:, None]
    d_diag = ii[None, :] - ii[:, None]
    bk_prev = _rel_bucket_np(d_prev).astype(np.float32)
    bk_diag = _rel_bucket_np(d_diag).astype(np.float32)
    cm_diag = np.where(d_diag < 0, np.float32(NEG), np.float32(0.0)).astype(np.float32)
    ident = np.eye(128, dtype=np.float32).astype(ml_dtypes.bfloat16)
    rows = []
    for r in range(NCORES):
        rows.append(np.arange(2048 + r * 256, 2048 + (r + 1) * 256))
    for h in range(2):
        for r in range(NCORES):
            rows.append(np.arange((2 * r + h) * 128, (2 * r + h + 1) * 128))
    rows = np.concatenate(rows)
    w_out_p = w_out[rows, :]
    in_maps = []
    for c in range(NCORES):
        cs = slice(c * 256, (c + 1) * 256)
        segs = [0, 2048, 8192, 12288, 10240, 14336, 4096, 6144]
        wc = np.concatenate([w_in[:, o + c * 256:o + (c + 1) * 256] for o in segs], axis=1)
        cw = conv_w[:, cs]
        convw = np.ascontiguousarray(cw.reshape(3, 2, 128).transpose(2, 1, 0).reshape(128, 6))
        relb = np.ascontiguousarray(np.broadcast_to(
            rel_bias[:, 2 * c:2 * c + 2].T.reshape(1, 64), (128, 64)))
        osl = slice(c * 512, (c + 1) * 512)
        in_maps.append({
            "x": x2,
            "x_cs": np.ascontiguousarray(x2[:, osl]),
            "w_in": np.ascontiguousarray(wc),
            "w_out": np.ascontiguousarray(w_out_p[:, osl]),
            "gcol": gcol,
            "convw": convw,
            "relb": relb,
            "fgs": np.ascontiguousarray(np.broadcast_to(fg[osl].reshape(1, 512), (128, 512))),
            "bk_prev": bk_prev, "bk_diag": bk_diag, "cm_diag": cm_diag,
            "ident": ident,
            "identf": np.eye(128, dtype=np.float32),
        })
    return in_maps


_NC_CACHE = {}


def kernel(x, norm_gain, w_in, conv_w, w_out, rel_bias, final_gain):
    in_maps = _host_inputs(x, norm_gain, w_in, conv_w, w_out, rel_bias, final_gain)
    if "nc" not in _NC_CACHE:
        _NC_CACHE["nc"] = build_nc()
    res = run_bass_kernel_spmd(_NC_CACHE["nc"], in_maps, core_ids=list(range(NCORES)))
    outs = [np.asarray(res.results[c]["out"], dtype=np.float32) for c in range(NCORES)]
    full = np.concatenate(outs, axis=1).reshape(1, S, DM)
    return full
```

```python
import math
from contextlib import ExitStack

import numpy as np
import ml_dtypes

import concourse.bass as bass
import concourse.mybir as mybir
from concourse.bass_utils import run_bass_kernel_spmd

F32 = mybir.dt.float32
BF16 = mybir.dt.bfloat16
ALU = mybir.AluOpType
AF = mybir.ActivationFunctionType

NCORES = 8
S = 8192
DM = 4096
KC = DM // 128
CH = 256
NCHUNK = S // CH
NBLK = 32
TOK_D = S // NCORES
EPS = 1e-6
SCALE = 128.0 ** -0.5
NEG = -30000.0


class Res:
    __slots__ = ("name", "w", "r", "dsem", "dkey", "dcnt", "excl")

    def __init__(self, name):
        self.name = name
        self.excl = False
        self.w = None
        self.r = {}
        self.dsem = None
        self.dkey = None
        self.dcnt = 0


class Eng:
    def __init__(self, e, sem, key, sync_self):
        self.e = e
        self.sem = sem
        self.key = key
        self.n = 0
        self.waited = {}
        self.sync_self = sync_self


class Trk:
    def __init__(self, nc, es, nsem=90):
        self.nc = nc
        self.sems = [es.enter_context(nc.semaphore(f"sm{i}")) for i in range(nsem)]
        self.next = 0
        self.engs = {}
        for name, e, ss in (("pe", nc.tensor, False), ("act", nc.scalar, True),
                            ("dve", nc.vector, True), ("pool", nc.gpsimd, True),
                            ("sp", nc.sync, True)):
            s, k = self.newsem()
            self.engs[name] = Eng(e, s, k, ss)
        self.dres = []
        self.limit = None
        self.nops = 0
        self.log = []

    def newsem(self):
        s = self.sems[self.next]
        self.next += 1
        return s, self.next

    def res(self, name, dma=False, excl=False):
        r = Res(name)
        r.excl = excl
        if dma:
            r.dsem, r.dkey = self.newsem()
            self.dres.append(r)
        return r

    def op(self, eng, fn, reads=(), writes=(), inc=True, dma=None):
        self.nops += 1
        if self.limit is not None and self.nops > self.limit:
            return None
        E = self.engs[eng]
        ex = [r for r in reads if r.excl]
        if ex:
            reads = [r for r in reads if not r.excl]
            writes = list(writes) + ex
        deps = {}

        def add(k, s, v):
            if k not in deps or deps[k][1] < v:
                deps[k] = (s, v)

        for r in reads:
            if r.w is not None:
                add(*r.w)
        for w in writes:
            if w.w is not None:
                add(*w.w)
            for k, (s, v) in w.r.items():
                add(k, s, v)
        for k, (s, v) in deps.items():
            if k == E.key and not E.sync_self:
                continue
            if E.waited.get(k, 0) >= v:
                continue
            E.e.wait_ge(s, v)
            E.waited[k] = v
        ins = fn()
        if dma is not None:
            dma.dcnt += 16
            ins.then_inc(dma.dsem, 16)
            ev = (dma.dkey, dma.dsem, dma.dcnt)
        elif inc:
            E.n += 1
            ins.then_inc(E.sem, 1)
            ev = (E.key, E.sem, E.n)
        else:
            ev = (E.key, E.sem, E.n + 1)
        for r in reads:
            k, s, v = ev
            if k not in r.r or r.r[k][1] < v:
                r.r[k] = (s, v)
        for w in writes:
            w.w = ev
            w.r = {}
        return ins

    def barrier(self, force=False):
        if self.limit is not None and self.nops > self.limit and not force:
            return
        for E in self.engs.values():
            for F in self.engs.values():
                if F is E or F.n == 0:
                    continue
                if E.waited.get(F.key, 0) < F.n:
                    E.e.wait_ge(F.sem, F.n)
                    E.waited[F.key] = F.n
            for r in self.dres:
                if r.dcnt and E.waited.get(r.dkey, 0) < r.dcnt:
                    E.e.wait_ge(r.dsem, r.dcnt)
                    E.waited[r.dkey] = r.dcnt


def build_nc(debug=False, phases="ABCD", ntok=S, limit=None):
    nc = bass.Bass("TRN2", target_bir_lowering=False)
    S = ntok
    NCHUNK = S // CH
    NT = S // 128
    NG = S // 512

    def dram_in(name, shape, dt):
        return nc.dram_tensor(name, shape, dt, kind="ExternalInput").ap()

    def dram_scr(name, shape, dt, dbg=True):
        if debug and dbg:
            return nc.dram_tensor(name, shape, dt, kind="ExternalOutput").ap()
        return nc.dram_tensor(name, shape, dt).ap()

    x = dram_in("x", [S, DM], F32)
    w_in = dram_in("w_in", [DM, 2048], F32)
    w_out = dram_in("w_out", [DM, 512], F32)
    x_cs = dram_in("x_cs", [S, 512], F32)
    gcol_d = dram_in("gcol", [128, KC], F32)
    convw_d = dram_in("convw", [128, 6], F32)
    relb_d = dram_in("relb", [128, 64], F32)
    fgs_d = dram_in("fgs", [128, 512], F32)
    bkp_d = dram_in("bk_prev", [128, 128], F32)
    bkd_d = dram_in("bk_diag", [128, 128], F32)
    cmd_d = dram_in("cm_diag", [128, 128], F32)
    ident_d = dram_in("ident", [128, 128], BF16)
    identf_d = dram_in("identf", [128, 128], F32)
    out = nc.dram_tensor("out", [S, 512], F32, kind="ExternalOutput").ap()
    ari_t = nc.dram_tensor("ar_in", [128, 64], F32)
    aro_t = nc.dram_tensor("ar_out", [128, 64], F32)

    qT_d = dram_scr("qT_s", [2, 128, S], BF16)
    kT_d = dram_scr("kT_s", [2, 128, S], BF16)
    v_d = dram_scr("v_s", [S, 256], BF16)
    sza_d = dram_scr("sza_s", [S, 256], F32)
    mask_d = dram_scr("mask_s", [S, 64], F32)
    gci_t = nc.dram_tensor("gc_in", [256, S], BF16)
    NPART = [1, 2]
    gh_t = [[nc.dram_tensor(f"gh{h}_{p}_in", [128, S // NPART[h]], BF16) for p in range(NPART[h])]
            for h in range(2)]
    gco_t = nc.dram_tensor("gc_out", [NCORES * 256, S], BF16)
    gho_t = [[nc.dram_tensor(f"gh{h}_{p}_out", [NCORES * 128, S // NPART[h]], BF16)
              for p in range(NPART[h])] for h in range(2)]
    if debug:
        dbg_gci = nc.dram_tensor("dbg_gci", [256, S], BF16, kind="ExternalOutput").ap()
        dbg_gh = [nc.dram_tensor(f"dbg_gh{h}", [128, S], BF16, kind="ExternalOutput").ap()
                  for h in range(2)]

    with ExitStack() as es:
        T = Trk(nc, es)
        T.limit = limit
        nc._trk = T
        op = T.op
        pe, act, dve, pool, sp = nc.tensor, nc.scalar, nc.vector, nc.gpsimd, nc.sync

        def sb(st, name, shape, dt):
            return st.enter_context(nc.sbuf_tensor("s_" + name, shape, dt))

        def ps(st, name, shape, dt):
            return st.enter_context(nc.psum_tensor("p_" + name, shape, dt))

        R_qT = [T.res(f"qT{h}") for h in range(2)]
        R_kT = [T.res(f"kT{h}") for h in range(2)]
        R_v = T.res("v_d")
        R_sza = T.res("sza_d")
        R_mask = T.res("mask_d")
        R_gci = T.res("gci")
        R_gh = [[T.res(f"gh{h}_{p}") for p in range(NPART[h])] for h in range(2)]
        R_gco = T.res("gco")
        R_gho = [[T.res(f"gho{h}_{p}") for p in range(NPART[h])] for h in range(2)]
        R_out = T.res("out")

        ident = sb(es, "ident", [128, 128], BF16)
        gcol = sb(es, "gcol", [128, KC], F32)
        convw = sb(es, "convw", [128, 6], F32)
        relb = sb(es, "relb", [128, 64], F32)
        neghalf = sb(es, "neghalf", [128, 1], F32)
        R_const = T.res("const", dma=True)
        R_nh = T.res("neghalf")
        for dst, src in ((ident, ident_d), (gcol, gcol_d), (convw, convw_d), (relb, relb_d)):
            op("sp", lambda dst=dst, src=src: sp.dma_start(out=dst[:], in_=src), writes=[R_const],
               dma=R_const)

        if "A" in phases:
          with ExitStack() as st:
            Wb = sb(st, "Wb", [128, KC, 2048], BF16)
            xh = [sb(st, f"xh{i}", [128, 2048], F32) for i in range(3)]
            xs = sb(st, "xs", [128, DM], BF16)
            uT = [sb(st, f"uT{i}", [128, KC, CH], BF16) for i in range(2)]
            ss = [sb(st, f"ss{i}", [128, 2], F32) for i in range(2)]
            ms = [sb(st, f"ms{i}", [128, 1], F32) for i in range(2)]
            rstd = [sb(st, f"rstd{i}", [128, 1], F32) for i in range(2)]
            qk_sb = sb(st, "qk_sb", [128, 512], BF16)
            QK = sb(st, "QK", [128, 4, CH], BF16)
            ksum4 = sb(st, "ksum4", [128, 4], F32)
            kmT = [sb(st, f"kmT{i}", [128, NBLK], BF16) for i in range(2)]
            inv = sb(st, "inv", [128, 2], BF16)
            identf = sb(st, "identf", [128, 128], F32)
            vs = [sb(st, f"vs{i}", [128, 256], BF16) for i in range(2)]
            gsb = [sb(st, f"gsb{i}", [128, NBLK], F32) for i in range(4)]
            top8 = sb(st, "top8", [128, 8], F32)
            thr = sb(st, "thr", [128, 1], F32)
            msk = [sb(st, f"msk{i}", [128, 64], F32) for i in range(2)]
            Csb = sb(st, "Csb", [128, 256], F32)
            tmf = sb(st, "tmf", [128, 256], F32)
            szc = sb(st, "szc", [128, 256], F32)
            t1 = sb(st, "t1", [128, 2, CH + 2], F32)
            yy = sb(st, "yy", [128, CH], F32)
            cv = sb(st, "cv", [128, CH], BF16)
            tp = [ps(st, f"tp{i}", [128, 8, 128], BF16) for i in range(2)]
            mm = [ps(st, f"mm{i}", [128, 512], F32) for i in range(3)]
            tqk = ps(st, "tqk", [128, 4, 128], BF16)
            tcv = ps(st, "tcv", [128, 512], F32)
            gTp = ps(st, "gTp", [128, 2, CH], F32)

            R_W = [T.res(f"W{k}") for k in range(KC)]
            R_xh = [T.res(f"xh{i}", dma=True) for i in range(3)]
            R_xs = [T.res("xs0"), T.res("xs1")]
            R_uT = [[T.res(f"uT{i}_{g}") for g in range(4)] for i in range(2)]
            R_ss = [T.res(f"ss{i}") for i in range(2)]
            R_ms = [T.res(f"ms{i}") for i in range(2)]
            R_rstd = [T.res(f"rstd{i}") for i in range(2)]
            R_qksb = T.res("qk_sb")
            R_QK = T.res("QK", dma=True)
            R_ksum4 = T.res("ksum4")
            R_kmT = [T.res(f"kmT{i}") for i in range(2)]
            R_inv = T.res("inv")
            R_identf = T.res("identf", dma=True)
            R_vs = [T.res(f"vs{i}", dma=True) for i in range(2)]
            R_gsb = [T.res(f"gsb{i}") for i in range(4)]
            R_top8 = T.res("top8")
            R_thr = T.res("thr")
            R_msk = [T.res(f"msk{i}", dma=True) for i in range(2)]
            R_Csb = T.res("Csb")
            R_tmf = T.res("tmf")
            R_szc = T.res("szc", dma=True)
            R_t1 = T.res("t1")
            R_yy = T.res("yy")
            R_cv = T.res("cv", dma=True)
            R_tp = [T.res(f"tp{i}", excl=True) for i in range(2)]
            R_mm = [T.res(f"mm{i}", excl=True) for i in range(3)]
            R_tqk = T.res("tqk", excl=True)
            R_tcv = T.res("tcv", excl=True)
            R_gTp = T.res("gTp", excl=True)

            for i in range(2):
                op("dve", lambda i=i: dve.memset(kmT[i][:], 0.0), writes=[R_kmT[i]])
            op("dve", lambda: dve.memset(t1[:], 0.0), writes=[R_t1])
            op("dve", lambda: dve.memset(inv[:], 1.0 / CH), writes=[R_inv])
            op("sp", lambda: sp.dma_start(out=identf[:], in_=identf_d), writes=[R_identf], dma=R_identf)
            for i in range(4):
                op("dve", lambda i=i: dve.memset(gsb[i][:], -1e30), writes=[R_gsb[i]])

            for k in range(KC):
                s_ = k % 3
                op("sp", lambda k=k, s_=s_: sp.dma_start(out=xh[s_][:], in_=w_in[k * 128:(k + 1) * 128, :]),
                   writes=[R_xh[s_]], dma=R_xh[s_])
                if k % 2 == 0:
                    op("dve", lambda k=k, s_=s_: dve.tensor_scalar(
                        out=Wb[:, k, :], in0=xh[s_][:], scalar1=gcol[:, k:k + 1], scalar2=None,
                        op0=ALU.mult), reads=[R_xh[s_], R_const], writes=[R_W[k]])
                else:
                    op("act", lambda k=k, s_=s_: act.activation(
                        out=Wb[:, k, :], in_=xh[s_][:], func=AF.Copy, scale=gcol[:, k:k + 1]),
                       reads=[R_xh[s_], R_const], writes=[R_W[k]])

            slot_ctr = [0]
            fm_ctr = [0]

            prep_slots = {}

            def prep_act(c, tt):
                i = 2 * c + tt
                b = i % 2
                slots = []
                for h in range(2):
                    s_ = slot_ctr[0] % 3
                    slot_ctr[0] += 1
                    slots.append(s_)
                    op("sp", lambda i=i, h=h, s_=s_: sp.dma_start(
                        out=xh[s_][:], in_=x[i * 128:(i + 1) * 128, h * 2048:(h + 1) * 2048]),
                       writes=[R_xh[s_]], dma=R_xh[s_])
                    op("act", lambda h=h, s_=s_, b=b: act.activation(
                        out=xs[:, h * 2048:(h + 1) * 2048], in_=xh[s_][:], func=AF.Square,
                        accum_out=ss[b][:, h:h + 1]),
                       reads=[R_xh[s_]], writes=[R_xs[h], R_ss[b]])
                op("dve", lambda b=b: dve.tensor_tensor(
                    out=ms[b][:], in0=ss[b][:, 0:1], in1=ss[b][:, 1:2], op=ALU.add),
                   reads=[R_ss[b]], writes=[R_ms[b]])
                op("act", lambda b=b: act.activation(
                    out=ms[b][:], in_=ms[b][:], func=AF.Sqrt, bias=EPS, scale=1.0 / DM),
                   reads=[R_ms[b]], writes=[R_ms[b]])
                op("dve", lambda b=b: dve.reciprocal(out=rstd[b][:], in_=ms[b][:]),
                   reads=[R_ms[b]], writes=[R_rstd[b]])
                for h in range(2):
                    s_ = slots[h]
                    op("act", lambda h=h, s_=s_, b=b: act.activation(
                        out=xs[:, h * 2048:(h + 1) * 2048], in_=xh[s_][:], func=AF.Copy,
                        scale=rstd[b][:, 0:1]),
                       reads=[R_xh[s_], R_rstd[b]], writes=[R_xs[h]])

            def prep_pe(c, tt):
                u = c % 2
                for kg in range(4):
                    pb = kg % 2
                    for kk in range(8):
                        k = kg * 8 + kk
                        op("pe", lambda k=k, kk=kk, pb=pb: pe.transpose(
                            out=tp[pb][:, kk, :], in_=xs[:, k * 128:(k + 1) * 128],
                            identity=ident[:]),
                           reads=[R_xs[k // 16], R_const], writes=[R_tp[pb]], inc=(kk == 7))
                    op("dve", lambda kg=kg, pb=pb, u=u, tt=tt: dve.tensor_copy(
                        out=uT[u][:, kg * 8:(kg + 1) * 8, tt * 128:(tt + 1) * 128],
                        in_=tp[pb][:]),
                       reads=[R_tp[pb]], writes=[R_uT[u][kg]])

            def inproj(c, hooks):
                u = c % 2
                deferred = []
                for f_ in hooks.get(-1, []):
                    f_()

                def mm_group(tt, n):
                    bi = fm_ctr[0] % 3
                    fm_ctr[0] += 1
                    for k in range(KC):
                        op("pe", lambda k=k, bi=bi: pe.matmul(
                            out=mm[bi][:, :], lhsT=uT[u][:, k, tt * 128:(tt + 1) * 128],
                            rhs=Wb[:, k, n * 512:(n + 1) * 512], start=(k == 0), stop=(k == KC - 1)),
                           reads=[R_W[k], R_uT[u][k // 8]], writes=[R_mm[bi]], inc=(k == KC - 1))
                    for f_ in hooks.get(tt * 4 + n, []):
                        f_()
                    return bi

                def flush():
                    while deferred:
                        deferred.pop(0)()

                for tt in range(2):
                    row0 = (2 * c + tt) * 128
                    tsl = slice(tt * 128, (tt + 1) * 128)
                    bi = mm_group(tt, 0)
                    flush()
                    op("dve", lambda bi=bi: dve.tensor_copy(out=qk_sb[:], in_=mm[bi][:, :]),
                       reads=[R_mm[bi]], writes=[R_qksb])

                    def d_qk(tt=tt, tsl=tsl):
                        for i in range(4):
                            op("pe", lambda i=i: pe.transpose(
                                out=tqk[:, i, :], in_=qk_sb[:, i * 128:(i + 1) * 128], identity=ident[:]),
                               reads=[R_qksb, R_const], writes=[R_tqk], inc=False)
                        for h in range(2):
                            op("pe", lambda h=h: pe.matmul(
                                out=tcv[:, 384 + tt * 2 + h:385 + tt * 2 + h],
                                lhsT=qk_sb[:, 256 + h * 128:256 + (h + 1) * 128], rhs=inv[:, 0:1],
                                start=True, stop=True),
                               reads=[R_qksb, R_inv], writes=[R_tcv], inc=(h == 1))
                        op("dve", lambda: dve.tensor_copy(out=QK[:, :, tsl], in_=tqk[:]),
                           reads=[R_tqk], writes=[R_QK])
                    deferred.append(d_qk)
                    bi = mm_group(tt, 1)
                    flush()
                    op("act", lambda bi=bi: act.activation(out=Csb[:], in_=mm[bi][:, 256:512], func=AF.Copy),
                       reads=[R_mm[bi]], writes=[R_Csb])
                    op("dve", lambda bi=bi: dve.tensor_tensor(
                        out=tmf[:], in0=mm[bi][:, 0:256], in1=Csb[:], op=ALU.mult),
                       reads=[R_mm[bi], R_Csb], writes=[R_tmf])

                    def d_t1(tsl=tsl, tt=tt):
                        for j in range(2):
                            op("pe", lambda j=j: pe.transpose(
                                out=tcv[:, j * 128:(j + 1) * 128], in_=tmf[:, j * 128:(j + 1) * 128],
                                identity=identf[:]),
                               reads=[R_tmf, R_identf], writes=[R_tcv], inc=(j == 1))
                        op("act", lambda: act.activation(
                            out=t1[:, :, 2 + tt * 128:2 + (tt + 1) * 128],
                            in_=tcv[:, 0:256].rearrange("p (j t) -> p j t", j=2), func=AF.Copy),
                           reads=[R_tcv], writes=[R_t1])
                    deferred.append(d_t1)
                    bi = mm_group(tt, 2)
                    flush()
                    op("act", lambda bi=bi: act.activation(out=szc[:], in_=mm[bi][:, 256:512], func=AF.Silu),
                       reads=[R_mm[bi]], writes=[R_szc])
                    op("dve", lambda bi=bi: dve.tensor_tensor(
                        out=tmf[:], in0=mm[bi][:, 0:256], in1=szc[:], op=ALU.mult),
                       reads=[R_mm[bi], R_szc], writes=[R_tmf])

                    def d_g(tsl=tsl):
                        for j in range(2):
                            op("pe", lambda j=j: pe.transpose(
                                out=gTp[:, j, tsl], in_=tmf[:, j * 128:(j + 1) * 128], identity=identf[:]),
                               reads=[R_tmf, R_identf], writes=[R_gTp], inc=(j == 1))
                    deferred.append(d_g)
                    bi = mm_group(tt, 3)
                    flush()
                    op("dve", lambda bi=bi, tt=tt: dve.tensor_copy(out=vs[tt][:], in_=mm[bi][:, 0:256]),
                       reads=[R_mm[bi]], writes=[R_vs[tt]])
                    op("pool", lambda tt=tt, row0=row0: pool.dma_start(
                        out=v_d[row0:row0 + 128, :], in_=vs[tt][:]),
                       reads=[R_vs[tt]], writes=[R_v], dma=R_vs[tt])
                    op("act", lambda bi=bi: act.activation(out=szc[:], in_=mm[bi][:, 256:512], func=AF.Silu),
                       reads=[R_mm[bi]], writes=[R_szc])
                    op("pool", lambda row0=row0: pool.dma_start(
                        out=sza_d[row0:row0 + 128, :], in_=szc[:]),
                       reads=[R_szc], writes=[R_sza], dma=R_szc)
                flush()
                for h in range(2):
                    op("pool", lambda h=h: pool.dma_start(
                        out=qT_d[h, :, c * CH:(c + 1) * CH], in_=QK[:, h, :]),
                       reads=[R_QK], writes=[R_qT[h]], dma=R_QK)
                    op("pool", lambda h=h: pool.dma_start(
                        out=kT_d[h, :, c * CH:(c + 1) * CH], in_=QK[:, 2 + h, :]),
                       reads=[R_QK], writes=[R_kT[h]], dma=R_QK)
                for r_ in (R_qT[0], R_qT[1], R_kT[0], R_kT[1]):
                    r_.w = (R_QK.dkey, R_QK.dsem, R_QK.dcnt)
                for h in range(2):
                    for tt in range(2):
                        gi = h * 2 + tt
                        op("pe", lambda h=h, tt=tt, gi=gi: pe.matmul(
                            out=tcv[:, 256 + gi * 32:256 + (gi + 1) * 32],
                            lhsT=QK[:, h, tt * 128:(tt + 1) * 128], rhs=kmT[h][:],
                            start=True, stop=True),
                           reads=[R_QK, R_kmT[h]], writes=[R_tcv], inc=(gi == 3))
                for tt in range(2):
                    for h in range(2):
                        gi = h * 2 + tt
                        if c >= 1:
                            op("dve", lambda gi=gi: dve.tensor_copy(
                                out=gsb[gi][:, 0:c], in_=tcv[:, 256 + gi * 32:256 + gi * 32 + c]),
                               reads=[R_tcv], writes=[R_gsb[gi]])
                        op("dve", lambda gi=gi: dve.max(out=top8[:], in_=gsb[gi][:]),
                           reads=[R_gsb[gi]], writes=[R_top8])
                        op("dve", lambda: dve.tensor_scalar(
                            out=thr[:], in0=top8[:, 2:3], scalar1=-1e29, scalar2=None, op0=ALU.max),
                           reads=[R_top8], writes=[R_thr])
                        op("dve", lambda gi=gi, h=h, tt=tt: dve.tensor_scalar(
                            out=msk[tt][:, h * 32:(h + 1) * 32], in0=gsb[gi][:],
                            scalar1=thr[:, 0:1], scalar2=None, op0=ALU.is_ge),
                           reads=[R_gsb[gi], R_thr], writes=[R_msk[tt]])
                    op("pool", lambda tt=tt: pool.dma_start(
                        out=mask_d[(2 * c + tt) * 128:(2 * c + tt + 1) * 128, :], in_=msk[tt][:]),
                       reads=[R_msk[tt]], writes=[R_mask], dma=R_msk[tt])
                op("dve", lambda: dve.tensor_copy(out=ksum4[:], in_=tcv[:, 384:388]),
                   reads=[R_tcv], writes=[R_ksum4])
                for h in range(2):
                    op("dve", lambda h=h: dve.tensor_tensor(
                        out=kmT[h][:, c:c + 1], in0=ksum4[:, h:h + 1], in1=ksum4[:, 2 + h:3 + h], op=ALU.add),
                       reads=[R_ksum4], writes=[R_kmT[h]])
                for j in range(2):
                    op("dve", lambda j=j: dve.tensor_scalar(
                        out=yy[:], in0=t1[:, j, 0:CH], scalar1=convw[:, j * 3:j * 3 + 1],
                        scalar2=None, op0=ALU.mult),
                       reads=[R_t1, R_const], writes=[R_yy])
                    for tap in (1, 2):
                        op("dve", lambda j=j, tap=tap: dve.scalar_tensor_tensor(
                            out=yy[:], in0=t1[:, j, tap:tap + CH],
                            scalar=convw[:, j * 3 + tap:j * 3 + tap + 1], in1=yy[:],
                            op0=ALU.mult, op1=ALU.add),
                           reads=[R_t1, R_const, R_yy], writes=[R_yy])
                    op("dve", lambda j=j: dve.tensor_tensor(
                        out=cv[:], in0=gTp[:, j, :], in1=yy[:], op=ALU.mult),
                       reads=[R_gTp, R_yy], writes=[R_cv])
                    op("pool", lambda j=j: pool.dma_start(
                        out=gci_t.ap()[j * 128:(j + 1) * 128, c * CH:(c + 1) * CH], in_=cv[:]),
                       reads=[R_cv], writes=[R_gci], dma=R_cv)
                op("dve", lambda: dve.tensor_copy(out=t1[:, :, 0:2], in_=t1[:, :, CH:CH + 2]),
                   reads=[R_t1], writes=[R_t1])

            for tt in range(2):
                prep_act(0, tt)
                prep_pe(0, tt)
            for c in range(NCHUNK):
                hooks = {}
                if c + 1 < NCHUNK:
                    n_ = c + 1
                    hooks = {-1: [lambda n_=n_: prep_act(n_, 0)],
                             1: [lambda n_=n_: prep_pe(n_, 0), lambda n_=n_: prep_act(n_, 1)],
                             5: [lambda n_=n_: prep_pe(n_, 1)]}
                inproj(c, hooks)
            T.barrier()


        if "B" in phases:
          with ExitStack() as st:
            KT2 = [sb(st, f"KT{i}", [128, S], BF16) for i in range(2)]
            QT2 = [sb(st, f"QT{i}", [128, S], BF16) for i in range(2)]
            VA2 = [sb(st, f"VA{i}", [128, NT, 130], BF16) for i in range(2)]
            MK2 = [sb(st, f"MK{i}", [128, NT, 32], F32) for i in range(2)]
            SZ2 = [sb(st, f"SZ{i}", [128, NT, 128], F32) for i in range(2)]
            NPT = 6
            PT = [sb(st, f"PT{i}", [128, 512], BF16) for i in range(NPT)]
            NSS = 4
            Ssb = [sb(st, f"Ssb{i}", [128, 128], F32) for i in range(NSS)]
            ACC = [sb(st, f"ACC{i}", [128, 4, 130], F32) for i in range(2)]
            rec = sb(st, "rec", [128, 1], F32)
            og = [sb(st, f"og{i}", [128, 128], BF16) for i in range(2)]
            aT = [sb(st, f"aT{i}", [128, 512], BF16) for i in range(4)]
            NSB = 3
            SB = [ps(st, f"SB{i}", [128, 512], F32) for i in range(NSB)]
            NOB = 3
            OB = [ps(st, f"OB{i}", [128, 512], F32) for i in range(NOB)]
            TPB = ps(st, "TPB", [128, 1024], BF16)

            R_KT2 = [T.res(f"KT{i}", dma=True) for i in range(2)]
            R_QT2 = [T.res(f"QT{i}", dma=True) for i in range(2)]
            R_VA2 = [T.res(f"VA{i}", dma=True) for i in range(2)]
            R_MK2 = [T.res(f"MK{i}", dma=True) for i in range(2)]
            R_SZ2 = [T.res(f"SZ{i}", dma=True) for i in range(2)]
            R_PT = [T.res(f"PT{i}") for i in range(NPT)]
            R_Ssb = [T.res(f"Ssb{i}") for i in range(NSS)]
            R_ACC = [[T.res(f"ACC{i}_{q}") for q in range(4)] for i in range(2)]
            R_rec = T.res("rec")
            R_og = [T.res(f"og{i}") for i in range(2)]
            R_aT = [T.res(f"aT{i}", dma=True) for i in range(4)]
            R_SB = [T.res(f"SB{i}", excl=True) for i in range(NSB)]
            R_OB = [T.res(f"OB{i}", excl=True) for i in range(NOB)]
            R_TPB = T.res("TPB", excl=True)
            TT = sb(st, "TT", [128, 4, 128], F32)
            R_TT = T.res("TT")
            if True:
                bk = sb(st, "bk", [128, 3, 128], F32)
                tmp = sb(st, "tt_tmp", [128, 2, 128], F32)
                R_bk = T.res("bk", dma=True)
                R_tmp = T.res("tt_tmp")
                for i, src in enumerate((bkp_d, bkd_d, cmd_d)):
                    op("sp", lambda i=i, src=src: sp.dma_start(out=bk[:, i, :], in_=src),
                       dma=R_bk)
                R_bk.w = (R_bk.dkey, R_bk.dsem, R_bk.dcnt)
                for h in range(2):
                    dstt = TT[:, h * 2:h * 2 + 2, :]
                    for b in range(32):
                        if b == 0:
                            op("dve", lambda dstt=dstt, h=h, b=b: dve.tensor_scalar(
                                out=dstt, in0=bk[:, 0:2, :], scalar1=float(b),
                                scalar2=relb[:, h * 32 + b:h * 32 + b + 1],
                                op0=ALU.is_equal, op1=ALU.mult),
                               reads=[R_bk, R_const], writes=[R_TT])
                        else:
                            op("dve", lambda h=h, b=b: dve.tensor_scalar(
                                out=tmp[:], in0=bk[:, 0:2, :], scalar1=float(b),
                                scalar2=relb[:, h * 32 + b:h * 32 + b + 1],
                                op0=ALU.is_equal, op1=ALU.mult),
                               reads=[R_bk, R_const], writes=[R_tmp])
                            op("dve", lambda dstt=dstt: dve.tensor_tensor(
                                out=dstt, in0=dstt, in1=tmp[:], op=ALU.add),
                               reads=[R_tmp, R_TT], writes=[R_TT])
                    op("dve", lambda h=h: dve.tensor_tensor(
                        out=TT[:, h * 2 + 1, :], in0=TT[:, h * 2 + 1, :], in1=bk[:, 2, :], op=ALU.add),
                       reads=[R_bk, R_TT], writes=[R_TT])

            for i in range(2):
                op("dve", lambda i=i: dve.memset(VA2[i][:, :, 128:130], 1.0), writes=[R_VA2[i]])
            v_r = v_d.rearrange("(t p) d -> p t d", p=128)
            m_r = mask_d.rearrange("(t p) m -> p t m", p=128)
            z_r = sza_d.rearrange("(t p) d -> p t d", p=128)

            ctr = {"pt": 0, "ss": 0, "sb": 0, "ob": 0, "og": 0, "sz": 0}

            def load_head(hh, extra=()):
                KT, QT, VA, MK = KT2[hh], QT2[hh], VA2[hh], MK2[hh]
                R_KT, R_QT, R_VA, R_MK = R_KT2[hh], R_QT2[hh], R_VA2[hh], R_MK2[hh]
                extra = list(extra)
                for q4 in range(4):
                    sl = slice(q4 * (S // 4), (q4 + 1) * (S // 4))
                    op("sp", lambda sl=sl: sp.dma_start(out=KT[:, sl], in_=kT_d[hh, :, sl]),
                       reads=[R_kT[hh]] + extra, writes=[R_KT], dma=R_KT)
                    op("sp", lambda sl=sl: sp.dma_start(out=QT[:, sl], in_=qT_d[hh, :, sl]),
                       reads=[R_qT[hh]], writes=[R_QT], dma=R_QT)
                for t8 in range(8):
                    op("sp", lambda t8=t8: sp.dma_start(
                        out=VA[:, t8 * (NT // 8):(t8 + 1) * (NT // 8), 0:128],
                        in_=v_r[:, t8 * (NT // 8):(t8 + 1) * (NT // 8), hh * 128:(hh + 1) * 128]),
                       reads=[R_v], writes=[R_VA], dma=R_VA)
                for t4 in range(4):
                    op("sp", lambda t4=t4: sp.dma_start(
                        out=MK[:, t4 * (NT // 4):(t4 + 1) * (NT // 4), :],
                        in_=m_r[:, t4 * (NT // 4):(t4 + 1) * (NT // 4), hh * 32:(hh + 1) * 32]),
                       reads=[R_mask], writes=[R_MK], dma=R_MK)
                for t4 in range(4):
                    op("sp", lambda t4=t4: sp.dma_start(
                        out=SZ2[hh][:, t4 * (NT // 4):(t4 + 1) * (NT // 4), :],
                        in_=z_r[:, t4 * (NT // 4):(t4 + 1) * (NT // 4), hh * 128:(hh + 1) * 128]),
                       reads=[R_sza], writes=[R_SZ2[hh]], dma=R_SZ2[hh])
                for r_ in (R_KT, R_QT, R_VA, R_MK, R_SZ2[hh]):
                    r_.w = (r_.dkey, r_.dsem, r_.dcnt)

            load_head(0)
            def ag_conv():
                op("pool", lambda: pool.collective_compute(
                    "AllGather", ALU.bypass, replica_groups=[list(range(NCORES))],
                    ins=[gci_t.ap()], outs=[gco_t.ap()]),
                   reads=[R_gci, R_KT2[0], R_QT2[0], R_VA2[0], R_MK2[0], R_SZ2[0]], writes=[R_gco])

            def ag_head(h_, part):
                op("pool", lambda: pool.collective_compute(
                    "AllGather", ALU.bypass, replica_groups=[list(range(NCORES))],
                    ins=[gh_t[h_][part].ap()], outs=[gho_t[h_][part].ap()]),
                   reads=[R_gh[h_][part]], writes=[R_gho[h_][part]])

            GLATE = min(3, NG - 1)
            cc_sched = {}
            if "C" in phases:
                cc_sched[(0, GLATE)] = [ag_conv]

            for hh in range(2):
                KT, QT, VA, MK = KT2[hh], QT2[hh], VA2[hh], MK2[hh]
                R_KT, R_QT, R_VA, R_MK = R_KT2[hh], R_QT2[hh], R_VA2[hh], R_MK2[hh]
                c31 = relb[:, hh * 32 + 31:hh * 32 + 32]
                TTp = TT[:, hh * 2 + 0, :]
                TTd = TT[:, hh * 2 + 1, :]

                items = []
                for g in range(NG):
                    for j in range(2 * g + 2):
                        kts = []
                        for half in range(2):
                            kt = 2 * j + half
                            segs = []
                            for qi in range(4):
                                qt = 4 * g + qi
                                if kt > qt:
                                    continue
                                if kt == qt:
                                    segs.append((qi, "diag"))
                                elif kt == qt - 1:
                                    segs.append((qi, "prev"))
                                else:
                                    segs.append((qi, "far"))
                            if segs:
                                kts.append((kt, segs))
                        items.append((g, j, kts))

                state = {}

                def qk_exp(item):
                    g, j, kts = item
                    outl = []
                    for kt, segs in kts:
                        q0 = segs[0][0]
                        ncol = (4 - q0) * 128
                        bi = ctr["sb"] % NSB
                        ctr["sb"] += 1
                        pi = ctr["pt"] % NPT
                        ctr["pt"] += 1
                        op("pe", lambda kt=kt, q0=q0, ncol=ncol, bi=bi, g=g: pe.matmul(
                            out=SB[bi][:, q0 * 128:512], lhsT=KT[:, kt * 128:(kt + 1) * 128],
                            rhs=QT[:, g * 512 + q0 * 128:(g + 1) * 512], start=True, stop=True),
                           reads=[R_KT, R_QT], writes=[R_SB[bi]])
                        far0 = None
                        for qi, kind in segs:
                            if kind == "far":
                                if far0 is None:
                                    far0 = qi
                                continue
                            si = ctr["ss"] % NSS
                            ctr["ss"] += 1
                            tt_ = TTd if kind == "diag" else TTp
                            cs = slice(qi * 128, (qi + 1) * 128)
                            op("dve", lambda bi=bi, si=si, tt_=tt_, cs=cs: dve.scalar_tensor_tensor(
                                out=Ssb[si][:], in0=SB[bi][:, cs], scalar=SCALE, in1=tt_,
                                op0=ALU.mult, op1=ALU.add),
                               reads=[R_SB[bi], R_TT], writes=[R_Ssb[si]])
                            op("act", lambda si=si, pi=pi, cs=cs: act.activation(
                                out=PT[pi][:, cs], in_=Ssb[si][:], func=AF.Exp),
                               reads=[R_Ssb[si]], writes=[R_PT[pi]])
                        if far0 is not None:
                            cs = slice(far0 * 128, 512)
                            op("act", lambda bi=bi, pi=pi, cs=cs: act.activation(
                                out=PT[pi][:, cs], in_=SB[bi][:, cs], func=AF.Exp, bias=c31,
                                scale=SCALE),
                               reads=[R_SB[bi], R_const], writes=[R_PT[pi]])
                        outl.append((kt, pi, [s[0] for s in segs]))
                    return outl

                def pv(item, ptl):
                    g, j, kts = item
                    a = g % 2
                    for qi in range(4):
                        qt = 4 * g + qi
                        parts = [(kt, pi) for kt, pi, qis in ptl if qi in qis]
                        if not parts:
                            continue
                        oi = ctr["ob"] % NOB
                        ctr["ob"] += 1
                        for n_, (kt, pi) in enumerate(parts):
                            op("pe", lambda kt=kt, pi=pi, oi=oi, qi=qi, n_=n_, L=len(parts): pe.matmul(
                                out=OB[oi][:, 0:129], lhsT=PT[pi][:, qi * 128:(qi + 1) * 128],
                                rhs=VA[:, kt, 0:129], start=(n_ == 0), stop=(n_ == L - 1)),
                               reads=[R_PT[pi], R_VA], writes=[R_OB[oi]], inc=(n_ == len(parts) - 1))
                        own = (j == qt // 2)
                        first = (g, qi) not in state
                        state[(g, qi)] = True
                        accv = ACC[a][:, qi, 0:129]
                        mcol = MK[:, qt, j:j + 1]
                        if first and own:
                            op("dve", lambda oi=oi, accv=accv: dve.tensor_copy(out=accv, in_=OB[oi][:, 0:129]),
                               reads=[R_OB[oi]], writes=[R_ACC[a][qi]])
                        elif first:
                            op("dve", lambda oi=oi, accv=accv, mcol=mcol: dve.tensor_scalar(
                                out=accv, in0=OB[oi][:, 0:129], scalar1=mcol, scalar2=None, op0=ALU.mult),
                               reads=[R_OB[oi], R_MK], writes=[R_ACC[a][qi]])
                        elif own:
                            op("dve", lambda oi=oi, accv=accv: dve.tensor_tensor(
                                out=accv, in0=OB[oi][:, 0:129], in1=accv, op=ALU.add),
                               reads=[R_OB[oi], R_ACC[a][qi]], writes=[R_ACC[a][qi]])
                        else:
                            op("dve", lambda oi=oi, accv=accv, mcol=mcol: dve.scalar_tensor_tensor(
                                out=accv, in0=OB[oi][:, 0:129], scalar=mcol, in1=accv,
                                op0=ALU.mult, op1=ALU.add),
                               reads=[R_OB[oi], R_MK, R_ACC[a][qi]], writes=[R_ACC[a][qi]])

                def finalize(g):
                    a = g % 2
                    a4 = g % 4
                    for qi in range(4):
                        o_ = ctr["og"] % 2
                        ctr["og"] += 1
                        op("dve", lambda a=a, qi=qi: dve.reciprocal(out=rec[:], in_=ACC[a][:, qi, 128:129]),
                           reads=[R_ACC[a][qi]], writes=[R_rec])
                        op("dve", lambda a=a, qi=qi, o_=o_: dve.scalar_tensor_tensor(
                            out=og[o_][:], in0=ACC[a][:, qi, 0:128], scalar=rec[:, 0:1],
                            in1=SZ2[hh][:, 4 * g + qi, :], op0=ALU.mult, op1=ALU.mult),
                           reads=[R_ACC[a][qi], R_rec, R_SZ2[hh]], writes=[R_og[o_]])
                        op("pe", lambda o_=o_, qi=qi: pe.transpose(
                            out=TPB[:, qi * 128:(qi + 1) * 128], in_=og[o_][:], identity=ident[:]),
                           reads=[R_og[o_], R_const], writes=[R_TPB])
                    op("dve", lambda a4=a4: dve.tensor_copy(out=aT[a4][:], in_=TPB[:, 0:512]),
                       reads=[R_TPB], writes=[R_aT[a4]])
                    gpp = NG // NPART[hh]
                    part, gl = g // gpp, g % gpp
                    op("pool", lambda a4=a4, gl=gl, part=part: pool.dma_start(
                        out=gh_t[hh][part].ap()[:, gl * 512:(gl + 1) * 512], in_=aT[a4][:]),
                       reads=[R_aT[a4]], writes=[R_gh[hh][part]], dma=R_aT[a4])
                    if "C" in phases and gl == gpp - 1:
                        if hh == 0 and part == NPART[0] - 1:
                            cc_sched.setdefault((1, GLATE), []).append(
                                lambda part=part: ag_head(0, part))
                        else:
                            ag_head(hh, part)
                    for f_ in cc_sched.pop((hh, g), []):
                        f_()

                prev = None
                pre_at = None
                if hh == 0:
                    gpre = max(1, (NG * 5) // 8)
                    pre_at = next((ix for ix, it in enumerate(items) if it[0] >= gpre), None)
                for idx in range(len(items) + 1):
                    cur = None
                    if pre_at is not None and idx == pre_at:
                        load_head(1, extra=[R_gco] if "C" in phases else [])
                    if idx < len(items):
                        cur = (items[idx], qk_exp(items[idx]))
                    if prev is not None:
                        pv(*prev)
                        g_prev = prev[0][0]
                        if prev[0][1] == 2 * g_prev + 1:
                            finalize(g_prev)
                    prev = cur

            T.barrier()

        if debug:
            R_dbg = T.res("dbg", dma=True)
            op("sp", lambda: sp.dma_start(out=dbg_gci, in_=gci_t.ap()), reads=[R_gci], writes=[R_dbg], dma=R_dbg)
            for h in range(2):
                if "B" not in phases:
                    continue
                for p in range(NPART[h]):
                    w_ = S // NPART[h]
                    op("sp", lambda h=h, p=p, w_=w_: sp.dma_start(
                        out=dbg_gh[h][:, p * w_:(p + 1) * w_], in_=gh_t[h][p].ap()),
                       reads=[R_gh[h][p]], writes=[R_dbg], dma=R_dbg)
            op("sp", lambda: sp.nop(), reads=[R_dbg])
            T.barrier()

        if "D" in phases:
          with ExitStack() as st:
            wob = sb(st, "wob", [128, KC, 512], BF16)
            MX = [sb(st, f"MX{i}", [128, KC, CH], BF16) for i in range(2)]
            H = sb(st, "H", [128, 64, 512], F32)
            XT = [sb(st, f"XT{i}", [128, 512], F32) for i in range(2)]
            junk = sb(st, "junk", [128, 512], BF16)
            ssq = sb(st, "ssq", [128, 64], F32)
            ssr = sb(st, "ssr", [128, 64], F32)
            rs = sb(st, "rs", [128, 64], F32)
            nh64 = sb(st, "nh64", [128, 64], F32)
            fgs = sb(st, "fgs", [128, 512], F32)
            OP = [ps(st, f"OP{i}", [128, 512], F32) for i in range(4)]

            R_wob = [T.res(f"wob{k}") for k in range(8)]
            R_wst = [T.res(f"wst{i}", dma=True) for i in range(3)]
            wst = [H[:, 48 + 4 * i:52 + 4 * i, :] for i in range(3)]
            R_MX = [[T.res(f"MX{i}_{p}", dma=True) for p in range(3)] for i in range(2)]
            R_H = [T.res(f"H{t}") for t in range(64)]
            R_XT = [T.res(f"XT{i}", dma=True) for i in range(2)]
            R_junk = T.res("junk")
            R_ssq = T.res("ssq", dma=True)
            R_ssr = T.res("ssr", dma=True)
            R_rs = T.res("rs")
            R_nh64 = T.res("nh64")
            R_fgs = T.res("fgs", dma=True)
            R_OP = [T.res(f"OP{i}", excl=True) for i in range(4)]
            R_ari = T.res("ari")
            R_aro = T.res("aro")
            R_Hst = [T.res(f"Hst{i}", dma=True) for i in range(8)]

            op("sp", lambda: sp.dma_start(out=fgs[:], in_=fgs_d), writes=[R_fgs], dma=R_fgs)
            op("dve", lambda: dve.memset(ssq[:], 0.0), writes=[R_ssq])
            wo_r = w_out.rearrange("(k p) n -> p k n", p=128)
            for k8 in range(8):
                s_ = k8 % 3
                hres = [R_H[48 + 4 * s_ + i] for i in range(4)]
                op("sp", lambda k8=k8, s_=s_: sp.dma_start(out=wst[s_], in_=wo_r[:, k8 * 4:(k8 + 1) * 4, :]),
                   writes=[R_wst[s_]] + hres, dma=R_wst[s_])
                if k8 % 2 == 0:
                    op("dve", lambda k8=k8, s_=s_: dve.tensor_copy(
                        out=wob[:, k8 * 4:(k8 + 1) * 4, :], in_=wst[s_]),
                       reads=[R_wst[s_]] + hres, writes=[R_wob[k8]])
                else:
                    op("act", lambda k8=k8, s_=s_: act.activation(
                        out=wob[:, k8 * 4:(k8 + 1) * 4, :], in_=wst[s_], func=AF.Copy),
                       reads=[R_wst[s_]] + hres, writes=[R_wob[k8]])

            gc_r = gco_t.ap().rearrange("(k p) s -> p k s", p=128)
            gh_r = [[gho_t[h][p].ap().rearrange("(k p) s -> p k s", p=128) for p in range(NPART[h])]
                    for h in range(2)]

            def load_mx(tg):
                b = tg % 2
                sl = slice(tg * CH, (tg + 1) * CH)
                op("sp", lambda: sp.dma_start(out=MX[b][:, 0:16, :], in_=gc_r[:, :, sl]),
                   reads=[R_gco], writes=[R_MX[b][0]], dma=R_MX[b][0])
                for h in range(2):
                    cpp = NCHUNK // NPART[h]
                    part, tl = tg // cpp, tg % cpp
                    slh = slice(tl * CH, (tl + 1) * CH)
                    op("sp", lambda h=h, part=part, slh=slh: sp.dma_start(
                        out=MX[b][:, 16 + 8 * h:24 + 8 * h, :], in_=gh_r[h][part][:, :, slh]),
                       reads=[R_gho[h][part]], writes=[R_MX[b][1 + h]], dma=R_MX[b][1 + h])

            xctr = [0]
            load_mx(0)
            for tg in range(32):
                b = tg % 2
                if tg + 1 < 32:
                    load_mx(tg + 1)
                for tt in range(2):
                    t = 2 * tg + tt
                    bi = t % 4
                    xi = xctr[0] % 2
                    xctr[0] += 1
                    op("sp", lambda t=t, xi=xi: sp.dma_start(out=XT[xi][:], in_=x_cs[t * 128:(t + 1) * 128, :]),
                       writes=[R_XT[xi]], dma=R_XT[xi])
                    for k in range(KC):
                        part = 0 if k < 16 else (1 if k < 24 else 2)
                        op("pe", lambda k=k, tt=tt, bi=bi, b=b: pe.matmul(
                            out=OP[bi][:, :], lhsT=MX[b][:, k, tt * 128:(tt + 1) * 128], rhs=wob[:, k, :],
                            start=(k == 0), stop=(k == KC - 1)),
                           reads=[R_MX[b][part], R_wob[k // 4]], writes=[R_OP[bi]], inc=(k == KC - 1))
                    op("dve", lambda t=t, bi=bi, xi=xi: dve.tensor_tensor(
                        out=H[:, t, :], in0=OP[bi][:, :], in1=XT[xi][:], op=ALU.add),
                       reads=[R_OP[bi], R_XT[xi]], writes=[R_H[t]])
                    op("act", lambda t=t: act.activation(
                        out=junk[:], in_=H[:, t, :], func=AF.Square, accum_out=ssq[:, t:t + 1]),
                       reads=[R_H[t]], writes=[R_junk, R_ssq])
            op("pool", lambda: pool.dma_start(out=ari_t.ap(), in_=ssq[:]), reads=[R_ssq], writes=[R_ari], dma=R_ssq)
            op("pool", lambda: pool.collective_compute(
                "AllReduce", ALU.add, replica_groups=[list(range(NCORES))],
                ins=[ari_t.ap()], outs=[aro_t.ap()]), reads=[R_ari], writes=[R_aro])
            op("sp", lambda: sp.dma_start(out=ssr[:], in_=aro_t.ap()), reads=[R_aro], writes=[R_ssr], dma=R_ssr)
            op("act", lambda: act.activation(out=ssr[:], in_=ssr[:], func=AF.Sqrt, bias=EPS, scale=1.0 / DM),
               reads=[R_ssr], writes=[R_ssr])
            op("dve", lambda: dve.reciprocal(out=rs[:], in_=ssr[:]), reads=[R_ssr], writes=[R_rs])
            o_r = out.rearrange("(t p) n -> p t n", p=128)
            for t8 in range(8):
                for tt in range(8):
                    t = t8 * 8 + tt
                    op("dve", lambda t=t: dve.scalar_tensor_tensor(
                        out=H[:, t, :], in0=H[:, t, :], scalar=rs[:, t:t + 1], in1=fgs[:],
                        op0=ALU.mult, op1=ALU.mult),
                       reads=[R_H[t], R_rs, R_fgs], writes=[R_H[t]])
                op("sp", lambda t8=t8: sp.dma_start(out=o_r[:, t8 * 8:(t8 + 1) * 8, :], in_=H[:, t8 * 8:(t8 + 1) * 8, :]),
                   reads=[R_H[t8 * 8 + i] for i in range(8)], writes=[R_out], dma=R_Hst[t8])
            T.barrier()
        T.barrier(force=True)
    return nc


def _rel_bucket_np(d):
    n = np.maximum(d, 0)
    nf = np.maximum(n, 1).astype(np.float32)
    large = 16 + (np.log(nf / np.float32(16)) / np.float32(math.log(128 / 16)) * np.float32(16)).astype(np.int32)
    large = np.minimum(large, 31)
    return np.where(n < 16, n, large)


def _host_inputs(x, norm_gain, w_in, conv_w, w_out, rel_bias, final_gain):
    x2 = np.ascontiguousarray(np.asarray(x, dtype=np.float32).reshape(S, DM))
    w_in = np.asarray(w_in, dtype=np.float32)[0]
    w_out = np.asarray(w_out, dtype=np.float32)[0]
    conv_w = np.asarray(conv_w, dtype=np.float32)[0]
    g = np.asarray(norm_gain, dtype=np.float32)[0]
    rel_bias = np.asarray(rel_bias, dtype=np.float32)
    fg = np.asarray(final_gain, dtype=np.float32)
    gcol = np.ascontiguousarray(g.reshape(KC, 128).T)
    ii = np.arange(128)
    d_prev = ii[None, :] + 128 - ii[:, None]
    d_diag = ii[None, :] - ii[:, None]
    bk_prev = _rel_bucket_np(d_prev).astype(np.float32)
    bk_diag = _rel_bucket_np(d_diag).astype(np.float32)
    cm_diag = np.where(d_diag < 0, np.float32(NEG), np.float32(0.0)).astype(np.float32)
    ident = np.eye(128, dtype=np.float32).astype(ml_dtypes.bfloat16)
    rows = []
    for r in range(NCORES):
        rows.append(np.arange(2048 + r * 256, 2048 + (r + 1) * 256))
    for h in range(2):
        for r in range(NCORES):
            rows.append(np.arange((2 * r + h) * 128, (2 * r + h + 1) * 128))
    rows = np.concatenate(rows)
    w_out_p = w_out[rows, :]
    in_maps = []
    for c in range(NCORES):
        cs = slice(c * 256, (c + 1) * 256)
        segs = [0, 2048, 8192, 12288, 10240, 14336, 4096, 6144]
        wc = np.concatenate([w_in[:, o + c * 256:o + (c + 1) * 256] for o in segs], axis=1)
        cw = conv_w[:, cs]
        convw = np.ascontiguousarray(cw.reshape(3, 2, 128).transpose(2, 1, 0).reshape(128, 6))
        relb = np.ascontiguousarray(np.broadcast_to(
            rel_bias[:, 2 * c:2 * c + 2].T.reshape(1, 64), (128, 64)))
        osl = slice(c * 512, (c + 1) * 512)
        in_maps.append({
            "x": x2,
            "x_cs": np.ascontiguousarray(x2[:, osl]),
            "w_in": np.ascontiguousarray(wc),
            "w_out": np.ascontiguousarray(w_out_p[:, osl]),
            "gcol": gcol,
            "convw": convw,
            "relb": relb,
            "fgs": np.ascontiguousarray(np.broadcast_to(fg[osl].reshape(1, 512), (128, 512))),
            "bk_prev": bk_prev, "bk_diag": bk_diag, "cm_diag": cm_diag,
            "ident": ident,
            "identf": np.eye(128, dtype=np.float32),
        })
    return in_maps


_NC_CACHE = {}


def kernel(x, norm_gain, w_in, conv_w, w_out, rel_bias, final_gain):
    in_maps = _host_inputs(x, norm_gain, w_in, conv_w, w_out, rel_bias, final_gain)
    if "nc" not in _NC_CACHE:
        _NC_CACHE["nc"] = build_nc()
    res = run_bass_kernel_spmd(_NC_CACHE["nc"], in_maps, core_ids=list(range(NCORES)))
    outs = [np.asarray(res.results[c]["out"], dtype=np.float32) for c in range(NCORES)]
    full = np.concatenate(outs, axis=1).reshape(1, S, DM)
    return full
```
